# Optimizing a Trainium2 kernel written in Bass

```python
import math
import jax
import jax.numpy as jnp
from jax import lax
import numpy as np

D_MODEL = 1024
BATCH = 32
SEQ = 256
DEPTH = 2
DEC_BATCH = 8
DEC_SEQ = 2048
PAST_LEN = 512

GRID_W = 64
N_EVEN = (DEPTH + 1) // 2
N_ODD = DEPTH // 2
N_MOD = 9
D_FF = 2816
Q_BLOCK = 128
ROPE_THETA = 10000.0
EPS = 1e-6
NEG_INF = -1e30

D_RNN = 512
RNN_BLOCKS = 8
RNN_BW = D_RNN // RNN_BLOCKS
CONV_W = 4
CONV_LEFT = 2
LRU_C = 8.0

DIFF_HEADS = 4
DIFF_HD = 64
DIFF_W = DIFF_HEADS * 2 * DIFF_HD

WIN_HEADS = 16
WIN_KV = 4
WIN_G = WIN_HEADS // WIN_KV
WIN_HD = 64
WINDOW = 128

EVEN_IN = 2 * D_RNN + 3 * DIFF_W
EVEN_MIX = D_RNN + DIFF_W
ODD_IN = (WIN_HEADS + 2 * WIN_KV) * WIN_HD
ODD_MIX = WIN_HEADS * WIN_HD

kernel_name = 'hybrid_diffusion_prefix_trunk_step'


def rmsnorm(x, g):
    xf = x.astype(jnp.float32)
    y = xf * lax.rsqrt(jnp.mean(xf * xf, axis=-1, keepdims=True) + EPS)
    return (y * g.astype(jnp.float32)).astype(x.dtype)


def modulate(x, shift, scale):
    return x * (1 + scale[:, None, :]) + shift[:, None, :]


def swiglu(x, w1, w3, w2):
    return (jax.nn.silu(x @ w1) * (x @ w3)) @ w2


def diff_lambda_init(layer):
    return 0.8 - 0.6 * math.exp(-0.3 * layer)


def grid_angles(n_tok, head_dim):
    rows = n_tok // GRID_W
    row = jnp.repeat(jnp.arange(rows), GRID_W).astype(jnp.float32)
    col = jnp.tile(jnp.arange(GRID_W), rows).astype(jnp.float32)
    half = head_dim // 2
    inv = ROPE_THETA ** (-jnp.arange(0, half, 2, dtype=jnp.float32) / half)
    return row[:, None] * inv[None, :], col[:, None] * inv[None, :]


def rope_rotate(x, ang):
    n = x.shape[-1] // 2
    x1, x2 = x[..., :n], x[..., n:]
    c = jnp.cos(ang).astype(x.dtype)
    s = jnp.sin(ang).astype(x.dtype)
    return jnp.concatenate([x1 * c - x2 * s, x2 * c + x1 * s], axis=-1)


def axial_rope(x, ang_row, ang_col):
    shape = (1, x.shape[1]) + (1,) * (x.ndim - 3) + (ang_row.shape[-1],)
    half = x.shape[-1] // 2
    return jnp.concatenate([rope_rotate(x[..., :half], ang_row.reshape(shape)),
                            rope_rotate(x[..., half:], ang_col.reshape(shape))], axis=-1)


def sweep_queries(fn, q):
    B, S = q.shape[0], q.shape[1]
    nb = S // Q_BLOCK
    qb = jnp.moveaxis(q.reshape((B, nb, Q_BLOCK) + q.shape[2:]), 1, 0)
    out = lax.map(lambda args: fn(args[0], args[1]), (jnp.arange(nb), qb))
    return jnp.moveaxis(out, 0, 1).reshape((B, S) + out.shape[3:])


def centred_dwconv(x, w, b):
    S = x.shape[1]
    xp = jnp.pad(x, ((0, 0), (CONV_LEFT, CONV_W - 1 - CONV_LEFT), (0, 0)))
    y = b
    for tap in range(CONV_W):
        y = y + xp[:, tap:tap + S] * w[tap]
    return y


def _lin_combine(left, right):
    a1, b1 = left
    a2, b2 = right
    return a1 * a2, a2 * b1 + b2


def rglru_scan(xc, wa, ba, wi, bi, lam, h0, reverse):
    B, S, _ = xc.shape
    xb = xc.reshape(B, S, RNN_BLOCKS, RNN_BW)
    r = jax.nn.sigmoid(jnp.einsum('bsnk,nkj->bsnj', xb, wa).reshape(B, S, D_RNN) + ba)
    i = jax.nn.sigmoid(jnp.einsum('bsnk,nkj->bsnj', xb, wi).reshape(B, S, D_RNN) + bi)
    log_a = -LRU_C * r.astype(jnp.float32) * jax.nn.softplus(-lam.astype(jnp.float32))
    a = jnp.exp(log_a)
    u = jnp.sqrt(-jnp.expm1(2.0 * log_a)) * (i * xc).astype(jnp.float32)
    A, Bc = lax.associative_scan(_lin_combine, (a, u), reverse=reverse, axis=1)
    h = A * h0.astype(jnp.float32)[:, None, :] + Bc
    last = h[:, 0] if reverse else h[:, -1]
    return h, last


def diff_attn_block(qb, k, v, lam):
    s = jnp.einsum('bqhmd,bthmd->bhmqt', qb, k).astype(jnp.float32)
    p = jax.nn.softmax(s, axis=-1)
    w = (p[:, :, 0] - lam * p[:, :, 1]).astype(v.dtype)
    return jnp.einsum('bhqt,bthe->bqhe', w, v)


def gqa_sink_block(qb, k, v, mask, sink):
    s = jnp.einsum('bqkgd,btkd->bkgqt', qb, k).astype(jnp.float32)
    if mask is not None:
        s = jnp.where(mask, s, NEG_INF)
    sink_b = sink[None, :, :, None]
    m = jnp.maximum(jnp.max(s, axis=-1), sink_b)
    p = jnp.exp(s - m[..., None])
    denom = jnp.sum(p, axis=-1) + jnp.exp(sink_b - m)
    w = (p / denom[..., None]).astype(v.dtype)
    return jnp.einsum('bkgqt,btkd->bqkgd', w, v)


def even_mixer(h, w_in, w_out, conv_w, conv_b, lru_wa, lru_ba, lru_wi, lru_bi, lru_lam,
               q_g, k_g, lam_vec, subln_g, lam_init, h0, ctx_kv, ang):
    B, S, _ = h.shape
    xr, gt, q, k, v = jnp.split(h @ w_in, [D_RNN, 2 * D_RNN, 2 * D_RNN + DIFF_W, 2 * D_RNN + 2 * DIFF_W], axis=-1)
    xc = centred_dwconv(xr, conv_w, conv_b)
    h_fwd, last_fwd = rglru_scan(xc, lru_wa[0], lru_ba[0], lru_wi[0], lru_bi[0], lru_lam[0], h0[:, 0], False)
    h_bwd, last_bwd = rglru_scan(xc, lru_wa[1], lru_ba[1], lru_wi[1], lru_bi[1], lru_lam[1], h0[:, 1], True)
    y_rnn = (h_fwd + h_bwd).astype(h.dtype) * jax.nn.gelu(gt)
    q = rmsnorm(q.reshape(B, S, DIFF_HEADS, 2, DIFF_HD), q_g)
    k = rmsnorm(k.reshape(B, S, DIFF_HEADS, 2, DIFF_HD), k_g)
    v = v.reshape(B, S, DIFF_HEADS, 2 * DIFF_HD)
    lf = lam_vec.astype(jnp.float32)
    lam = jnp.exp(jnp.sum(lf[0] * lf[1])) - jnp.exp(jnp.sum(lf[2] * lf[3])) + lam_init
    if ctx_kv is None:
        k_all, v_all = k, v
        new_ctx = (k.reshape(B, S, DIFF_HEADS, 2 * DIFF_HD), v,
                   jnp.stack([last_fwd, last_bwd], axis=1).astype(h.dtype))
    else:
        q = axial_rope(q, ang[0], ang[1])
        k = axial_rope(k, ang[0], ang[1])
        ck, cv = ctx_kv
        k_all = jnp.concatenate([ck.reshape(B, ck.shape[1], DIFF_HEADS, 2, DIFF_HD), k], axis=1)
        v_all = jnp.concatenate([cv, v], axis=1)
        new_ctx = None
    q = q * (DIFF_HD ** -0.5)
    o = sweep_queries(lambda bi_, qb: diff_attn_block(qb, k_all, v_all, lam), q)
    o = rmsnorm(o, subln_g) * (1.0 - lam_init)
    out = jnp.concatenate([y_rnn, o.reshape(B, S, DIFF_W)], axis=-1) @ w_out
    return out, new_ctx


def odd_mixer(h, w_in, w_out, q_g, k_g, sink, ctx_kv, ang):
    B, S, _ = h.shape
    q, k, v = jnp.split(h @ w_in, [WIN_HEADS * WIN_HD, (WIN_HEADS + WIN_KV) * WIN_HD], axis=-1)
    q = rmsnorm(q.reshape(B, S, WIN_KV, WIN_G, WIN_HD), q_g)
    k = rmsnorm(k.reshape(B, S, WIN_KV, WIN_HD), k_g)
    v = v.reshape(B, S, WIN_KV, WIN_HD)
    sk = sink.reshape(WIN_KV, WIN_G).astype(jnp.float32)
    scale = WIN_HD ** -0.5
    if ctx_kv is None:
        q = q * scale
        o = sweep_queries(lambda bi_, qb: gqa_sink_block(qb, k, v, None, sk), q)
        new_ctx = (k, v)
    else:
        q = axial_rope(q, ang[0], ang[1]) * scale
        k = axial_rope(k, ang[0], ang[1])
        ck, cv = ctx_kv
        L = ck.shape[1]
        pad = ((0, 0), (Q_BLOCK, Q_BLOCK), (0, 0), (0, 0))
        kp = jnp.pad(k, pad)
        vp = jnp.pad(v, pad)
        span = 3 * Q_BLOCK

        def band_block(bi_, qb):
            start = bi_ * Q_BLOCK
            kl = lax.dynamic_slice_in_dim(kp, start, span, axis=1)
            vl = lax.dynamic_slice_in_dim(vp, start, span, axis=1)
            qpos = start + jnp.arange(Q_BLOCK)
            kpos = start - Q_BLOCK + jnp.arange(span)
            valid = (jnp.abs(qpos[:, None] - kpos[None, :]) <= WINDOW) & (kpos >= 0)[None, :] & (kpos < S)[None, :]
            mask = jnp.concatenate([valid, jnp.ones((Q_BLOCK, L), dtype=bool)], axis=1)
            return gqa_sink_block(qb, jnp.concatenate([kl, ck], axis=1), jnp.concatenate([vl, cv], axis=1), mask, sk)

        o = sweep_queries(band_block, q)
        new_ctx = None
    return o.reshape(B, S, ODD_MIX) @ w_out, new_ctx


def setup_inputs(seed: int = 0) -> dict:
    key = jax.random.key(seed)
    ks = iter(jax.random.split(key, 48))

    def nrm(shape, scale):
        return scale * jax.random.normal(next(ks), shape, jnp.float32)

    def gain(shape):
        return 1.0 + nrm(shape, 0.02)

    u = jax.random.uniform(next(ks), (N_EVEN, 2, D_RNN), jnp.float32, 0.9, 0.999)
    return {
        'x_prompt': nrm((BATCH, SEQ, D_MODEL), 1.0),
        'x_sample': nrm((DEC_BATCH, DEC_SEQ, D_MODEL), 1.0),
        'cache_diff_k': nrm((DEC_BATCH, N_EVEN, PAST_LEN, DIFF_HEADS, 2 * DIFF_HD), 1.0),
        'cache_diff_v': nrm((DEC_BATCH, N_EVEN, PAST_LEN, DIFF_HEADS, 2 * DIFF_HD), 1.0),
        'state_lru': nrm((DEC_BATCH, N_EVEN, 2, D_RNN), 0.5),
        'cache_win_k': nrm((DEC_BATCH, N_ODD, PAST_LEN, WIN_KV, WIN_HD), 1.0),
        'cache_win_v': nrm((DEC_BATCH, N_ODD, PAST_LEN, WIN_KV, WIN_HD), 1.0),
        'c': nrm((DEC_BATCH, D_MODEL), 1.0),
        'c_ctx': nrm((D_MODEL,), 1.0),
        'norm_g': gain((DEPTH, 3, D_MODEL)),
        'w_mod': nrm((DEPTH, D_MODEL, N_MOD * D_MODEL), 0.5 * D_MODEL ** -0.5),
        'b_mod': nrm((DEPTH, N_MOD * D_MODEL), 0.01),
        'ffn_w1': nrm((DEPTH, 2, D_MODEL, D_FF), D_MODEL ** -0.5),
        'ffn_w3': nrm((DEPTH, 2, D_MODEL, D_FF), D_MODEL ** -0.5),
        'ffn_w2': nrm((DEPTH, 2, D_FF, D_MODEL), D_FF ** -0.5),
        'e_w_in': nrm((N_EVEN, D_MODEL, EVEN_IN), D_MODEL ** -0.5),
        'e_w_out': nrm((N_EVEN, EVEN_MIX, D_MODEL), EVEN_MIX ** -0.5),
        'e_conv_w': nrm((N_EVEN, CONV_W, D_RNN), CONV_W ** -0.5),
        'e_conv_b': nrm((N_EVEN, D_RNN), 0.01),
        'e_lru_wa': nrm((N_EVEN, 2, RNN_BLOCKS, RNN_BW, RNN_BW), RNN_BW ** -0.5),
        'e_lru_ba': nrm((N_EVEN, 2, D_RNN), 0.01),
        'e_lru_wi': nrm((N_EVEN, 2, RNN_BLOCKS, RNN_BW, RNN_BW), RNN_BW ** -0.5),
        'e_lru_bi': nrm((N_EVEN, 2, D_RNN), 0.01),
        'e_lru_lam': jnp.log(u) - jnp.log1p(-u),
        'e_q_g': gain((N_EVEN, DIFF_HD)),
        'e_k_g': gain((N_EVEN, DIFF_HD)),
        'e_lam': nrm((N_EVEN, 4, DIFF_HD), 0.1),
        'e_subln_g': gain((N_EVEN, 2 * DIFF_HD)),
        'o_w_in': nrm((N_ODD, D_MODEL, ODD_IN), D_MODEL ** -0.5),
        'o_w_out': nrm((N_ODD, ODD_MIX, D_MODEL), ODD_MIX ** -0.5),
        'o_q_g': gain((N_ODD, WIN_HD)),
        'o_k_g': gain((N_ODD, WIN_HD)),
        'o_sink': nrm((N_ODD, WIN_HEADS), 1.0),
    }


def reference(x_prompt, x_sample, cache_diff_k, cache_diff_v, state_lru, cache_win_k, cache_win_v, c,
              c_ctx, norm_g, w_mod, b_mod, ffn_w1, ffn_w3, ffn_w2,
              e_w_in, e_w_out, e_conv_w, e_conv_b, e_lru_wa, e_lru_ba, e_lru_wi, e_lru_bi, e_lru_lam,
              e_q_g, e_k_g, e_lam, e_subln_g,
              o_w_in, o_w_out, o_q_g, o_k_g, o_sink):

    def trunk(x, cond, latent):
        B, S, _ = x.shape
        ang = grid_angles(S, DIFF_HD) if latent else None
        ctx_out = ([], [], [], [], [])
        for l in range(DEPTH):
            j = l // 2
            mod = jax.nn.silu(cond) @ w_mod[l] + b_mod[l]
            sh1, sc1, g1, sh2, sc2, g2, sh3, sc3, g3 = jnp.split(mod, N_MOD, axis=-1)
            x = x + 0.5 * g1[:, None] * swiglu(modulate(rmsnorm(x, norm_g[l, 0]), sh1, sc1),
                                               ffn_w1[l, 0], ffn_w3[l, 0], ffn_w2[l, 0])
            h = modulate(rmsnorm(x, norm_g[l, 1]), sh2, sc2)
            if l % 2 == 0:
                if latent:
                    h0 = state_lru[:, j]
                    ctx_kv = (cache_diff_k[:, j], cache_diff_v[:, j])
                else:
                    h0 = jnp.zeros((B, 2, D_RNN), jnp.float32)
                    ctx_kv = None
                out, new_ctx = even_mixer(h, e_w_in[j], e_w_out[j], e_conv_w[j], e_conv_b[j],
                                          e_lru_wa[j], e_lru_ba[j], e_lru_wi[j], e_lru_bi[j], e_lru_lam[j],
                                          e_q_g[j], e_k_g[j], e_lam[j], e_subln_g[j], diff_lambda_init(l),
                                          h0, ctx_kv, ang)
                if not latent:
                    ctx_out[0].append(new_ctx[0])
                    ctx_out[1].append(new_ctx[1])
                    ctx_out[2].append(new_ctx[2])
            else:
                ctx_kv = (cache_win_k[:, j], cache_win_v[:, j]) if latent else None
                out, new_ctx = odd_mixer(h, o_w_in[j], o_w_out[j], o_q_g[j], o_k_g[j], o_sink[j], ctx_kv, ang)
                if not latent:
                    ctx_out[3].append(new_ctx[0])
                    ctx_out[4].append(new_ctx[1])
            x = x + g2[:, None] * out
            x = x + 0.5 * g3[:, None] * swiglu(modulate(rmsnorm(x, norm_g[l, 2]), sh3, sc3),
                                               ffn_w1[l, 1], ffn_w3[l, 1], ffn_w2[l, 1])
        return x, ctx_out

    y_prompt, ctx_out = trunk(x_prompt, c_ctx[None, :], False)
    y_sample, _ = trunk(x_sample, c, True)
    new_diff_k = jnp.stack(ctx_out[0], axis=1)
    new_diff_v = jnp.stack(ctx_out[1], axis=1)
    new_state_lru = jnp.stack(ctx_out[2], axis=1)
    new_win_k = jnp.stack(ctx_out[3], axis=1)
    new_win_v = jnp.stack(ctx_out[4], axis=1)
    return (y_prompt, y_sample, new_diff_k, new_diff_v, new_state_lru, new_win_k, new_win_v)
```

```python
import math
import os
import numpy as np
import concourse.bass as bass
import concourse.mybir as mybir
from concourse.bass_utils import run_bass_kernel_spmd
from contextlib import ExitStack

F32 = mybir.dt.float32
BF16 = mybir.dt.bfloat16
F32R = mybir.dt.float32r
AF = mybir.ActivationFunctionType
ALU = mybir.AluOpType

D = 1024
DFF = 2816
NFC = 22
EPS = 1e-6
TS, TP = 2048, 1024
TT = TS + TP
DEBUG = bool(int(os.environ.get("MK_DEBUG", "0")))
STAGES = os.environ.get("MK_STAGES", "all")


class DSem:
    __slots__ = ("sem", "count")

    def __init__(self, sem):
        self.sem = sem
        self.count = 0


class Dummy:
    def __getitem__(self, idx):
        return self

    def __getattr__(self, name):
        return self

    def __call__(self, *a, **k):
        return self


class V:
    __slots__ = ("ap", "buf")

    def __init__(self, ap, buf):
        self.ap = ap
        self.buf = buf

    def __getitem__(self, idx):
        return V(self.ap[idx], self.buf)


class Buf:
    __slots__ = ("name", "base", "w", "readers", "dsem")

    def __init__(self, name, base, dsem=None):
        self.name = name
        self.base = base
        self.w = None
        self.readers = {}
        self.dsem = dsem

    def __getitem__(self, idx):
        return V(self.base[idx], self)

    @property
    def v(self):
        return V(self.base, self)


class K:
    ENG = ("pe", "act", "dve", "pool", "sp")

    def __init__(self, nc, stack):
        self.nc = nc
        self.stack = stack
        self.dry = False
        self.sem = {e: stack.enter_context(nc.semaphore("c_" + e)) for e in self.ENG}
        self.cnt = {e: 0 for e in self.ENG}
        self.known = {e: {} for e in self.ENG}
        self.dsems = []
        self.dsem_by_name = {}
        self.nwait = 0
        self.prog = {e: [] for e in self.ENG}
        self.uid = 0

    def new_dsem(self, name):
        if name in self.dsem_by_name:
            return self.dsem_by_name[name]
        d = DSem(self.stack.enter_context(self.nc.semaphore("d_" + name)))
        self.dsems.append(d)
        self.dsem_by_name[name] = d
        return d

    def sbuf(self, st, name, shape, dtype):
        if self.dry:
            return Dummy()
        self.uid += 1
        return st.enter_context(self.nc.sbuf_tensor(f"{name}_{self.uid}", list(shape), dtype))

    def psum(self, st, name, shape, dtype):
        if self.dry:
            return Dummy()
        self.uid += 1
        return st.enter_context(self.nc.psum_tensor(f"{name}_{self.uid}", list(shape), dtype))

    def buf(self, st, name, shape, dtype, dsem=None):
        t = self.sbuf(st, name, shape, dtype)
        return Buf(name, t[:] if not self.dry else t, self.new_dsem(dsem) if dsem else None)

    def _wait(self, e, sem, val):
        kn = self.known[e]
        if kn.get(id(sem), 0) >= val:
            return
        self.prog[e].append(lambda eng, sem=sem, val=val: eng.wait_ge(sem, val))
        kn[id(sem)] = val
        self.nwait += 1

    def _sync(self, e, reads, writes):
        needs = {}

        def add(tok):
            s, v = tok[0], tok[1]
            if len(tok) > 2:
                v = tok[2].count
            cur = needs.get(id(s))
            if cur is None or cur[1] < v:
                needs[id(s)] = (s, v)

        for b in reads:
            if b.w is not None:
                add(b.w)
        for b in writes:
            if b.w is not None:
                add(b.w)
            for tok in b.readers.values():
                add(tok)
        own = self.sem[e]
        for s, v in needs.values():
            if e == "pe" and s is own:
                continue
            self._wait(e, s, v)

    def _post(self, tok, reads, writes):
        s = tok[0]
        for b in reads:
            if b not in writes:
                b.readers[id(s)] = tok
        for b in writes:
            b.w = tok
            b.readers = {}

    def op(self, e, meth, **kw):
        if self.dry:
            return
        reads, writes = list(kw.pop("_r", [])), list(kw.pop("_w", []))
        kw2 = {}
        for key, v in kw.items():
            if isinstance(v, V):
                if key in ("out", "accum_out"):
                    if v.buf not in writes:
                        writes.append(v.buf)
                else:
                    if v.buf not in reads:
                        reads.append(v.buf)
                kw2[key] = v.ap
            else:
                kw2[key] = v
        self._sync(e, reads, writes)
        self.cnt[e] += 1
        sem = self.sem[e]
        self.prog[e].append(lambda eng, meth=meth, kw2=kw2, sem=sem: getattr(eng, meth)(**kw2).then_inc(sem, 1))
        self._post((sem, self.cnt[e]), reads, writes)

    def dma(self, e, out, in_, **kw):
        if self.dry:
            return
        if isinstance(out, V):
            buf = out.buf
            self._sync(e, [], [buf])
            o_, i_ = out.ap, in_
        else:
            buf = in_.buf
            self._sync(e, [buf], [])
            o_, i_ = out, in_.ap
        d = buf.dsem
        assert d is not None, buf.name
        d.count += 16
        self.prog[e].append(lambda eng, o_=o_, i_=i_, kw=kw, sem=d.sem: eng.dma_start(out=o_, in_=i_, **kw).then_inc(sem, 16))
        tok = (d.sem, d.count, d)
        if isinstance(out, V):
            self._post(tok, [], [buf])
        else:
            self._post(tok, [buf], [])

    def barrier(self):
        if self.dry:
            return
        for e in ("pe", "act", "dve", "pool"):
            if self.cnt[e]:
                self._wait("sp", self.sem[e], self.cnt[e])
        for d in self.dsems:
            if d.count:
                self._wait("sp", d.sem, d.count)
        self.cnt["sp"] += 1
        self.prog["sp"].append(lambda eng, sem=self.sem["sp"]: eng.nop().then_inc(sem, 1))
        for e in ("pe", "act", "dve", "pool"):
            self._wait(e, self.sem["sp"], self.cnt["sp"])
        for e in self.ENG:
            kn = self.known[e]
            for e2 in self.ENG:
                kn[id(self.sem[e2])] = self.cnt[e2]
            for d in self.dsems:
                kn[id(d.sem)] = d.count

    def finish(self):
        for e in ("pe", "act", "dve", "pool"):
            if self.cnt[e]:
                self._wait("sp", self.sem[e], self.cnt[e])
        for d in self.dsems:
            if d.count:
                self._wait("sp", d.sem, d.count)

    def emit(self):
        with self.nc.Block() as block:
            @block.sync
            def _(eng):
                for f in self.prog["sp"]:
                    f(eng)

            @block.tensor
            def _(eng):
                for f in self.prog["pe"]:
                    f(eng)

            @block.scalar
            def _(eng):
                for f in self.prog["act"]:
                    f(eng)

            @block.vector
            def _(eng):
                for f in self.prog["dve"]:
                    f(eng)

            @block.gpsimd
            def _(eng):
                for f in self.prog["pool"]:
                    f(eng)


class WS:
    LOOK = 6

    def __init__(self, k):
        self.k = k
        self.plan = []
        self.n = 0
        self.pos = 0
        self.rings = {}
        self.ring_cur = {}
        self.ring_cnt = {}
        self.epoch = {}

    def add_ring(self, name, slots):
        self.rings[name] = slots
        self.epoch[name] = self.epoch.get(name, 0) + 1

    def start_real(self):
        self.n = 0
        self.pos = 0
        self.ring_cur = {r: 0 for r in self.ring_cnt}
        self.ring_cnt = {r: 0 for r in self.ring_cnt}
        self.epoch = {}

    def get(self, ring, src, shape_idx=None):
        k = self.k
        if k.dry:
            seq = self.ring_cnt.get(ring, 0)
            self.ring_cnt[ring] = seq + 1
            self.plan.append((ring, seq, src, shape_idx, self.epoch[ring]))
            return Buf("dummy", Dummy())
        n = self.n
        self.n += 1
        ring_, seq, _, _, _ = self.plan[n]
        assert ring_ == ring
        self.ring_cur[ring] = seq
        while self.pos < len(self.plan) and self.pos <= n + self.LOOK:
            r, s, src_, sidx, ep = self.plan[self.pos]
            if r not in self.rings or self.epoch.get(r) != ep:
                break
            slots = self.rings[r]
            if s >= self.ring_cur[r] + len(slots):
                break
            slot = slots[s % len(slots)]
            dst = slot.v if sidx is None else slot[sidx]
            k.dma("pool", dst, src_, max_dma_last_dim=8192)
            self.pos += 1
        assert self.pos > n
        slots = self.rings[ring]
        return slots[seq % len(slots)]


def rope_tables():
    t = np.arange(TS)
    row = (t // 64).astype(np.float64)
    col = (t % 64).astype(np.float64)
    inv = 10000.0 ** (-np.arange(0, 32, 2, dtype=np.float64) / 32)
    C = np.zeros((128, TS), np.float32)
    S = np.zeros((128, TS), np.float32)
    for p in range(128):
        hd = p % 64
        half = hd // 32
        j = hd % 16
        pos = row if half == 0 else col
        ang = (pos.astype(np.float32) * np.float32(inv[j].astype(np.float32))).astype(np.float32)
        C[p] = np.cos(ang).astype(np.float32)
        S[p] = np.sin(ang).astype(np.float32)
    return C, S


def rot_matrix():
    R = np.zeros((128, 128), np.float32)
    for j in range(128):
        b = (j // 32) * 32
        jj = j % 32
        if jj < 16:
            R[b + jj + 16, j] = -1.0
        else:
            R[b + jj - 16, j] = 1.0
    return R


def band_mask():
    k = np.arange(128)[:, None]
    q = np.arange(-128, 256)[None, :]
    return (np.abs(q - k) <= 128).astype(np.float32)


VEC = {}
_off = 0


def _vadd(name, n):
    global _off
    VEC[name] = (_off, n)
    _off += n


_vadd("norm_g", 2 * 3 * 8)
_vadd("bmod", 2 * 72)
_vadd("cond", 16)
_vadd("conv_w", 16)
_vadd("conv_b", 4)
_vadd("lru_ba", 8)
_vadd("lru_bi", 8)
_vadd("lru_lam", 8)
_vadd("h0", 8)
_vadd("e_q_g", 1)
_vadd("e_k_g", 1)
_vadd("subln_g", 1)
_vadd("e_lam", 4)
_vadd("o_q_g", 1)
_vadd("o_k_g", 1)
_vadd("sink", 8)
NVEC = _off


def vslice(vt, name, lo=0, n=None):
    o, cnt = VEC[name]
    if n is None:
        n = cnt - lo
    return vt[:, o + lo:o + lo + n]


class Prog:
    def __init__(self):
        self.nc = bass.Bass("TRN2", target_bir_lowering=False)
        nc = self.nc
        self.dr = {}

        def din(name, shape):
            self.dr[name] = nc.dram_tensor(name, list(shape), F32, kind="ExternalInput").ap()

        def dout(name, shape):
            self.dr[name] = nc.dram_tensor(name, list(shape), F32, kind="ExternalOutput").ap()

        din("xin", [8, 128, TT])
        din("vecs", [128, NVEC])
        din("wmod", [2, 36, 128, 8, 256])
        din("w13", [2, 2, NFC, 128, 2, 8, 128])
        din("w2", [2, 2, 8, 128, NFC, 128])
        din("ewin", [16, 128, 8, 128])
        din("ewv", [128, 8, 512])
        din("ewout", [8, 128, 8, 128])
        din("lruw", [128, 16, 128])
        din("owin", [12, 128, 8, 128])
        din("owv", [128, 8, 256])
        din("owout", [8, 128, 8, 128])
        din("cdk", [128, 4, 512])
        din("cdv", [128, 4, 512])
        din("cwk", [128, 4, 512])
        din("cwv", [128, 4, 256])
        din("ropec", [128, TS])
        din("ropes", [128, TS])
        din("rmat", [128, 128])
        din("band", [128, 384])
        dout("yout", [8, 128, TT])
        dout("ndk", [128, 4, TP])
        dout("ndv", [128, 8, 512])
        dout("nst", [128, 32])
        dout("nwk", [64, 4, TP])
        dout("nwv", [128, 8, 256])
        if DEBUG:
            dout("dbg", [12, 8, 128, TT])

    def build(self):
        with ExitStack() as st:
            k = K(self.nc, st)
            ws = WS(k)
            self.k, self.ws = k, ws
            k.dry = True
            self.run(st)
            k.dry = False
            ws.start_real()
            self.run(st)
            k.finish()
            k.emit()
            self.stats = (k.nwait, dict(k.cnt), len(ws.plan))
        return self.nc

    def run(self, st):
        k, ws, dr = self.k, self.ws, self.dr
        self.vt = k.buf(st, "vecs", [128, NVEC], F32, "const")
        self.ones = k.buf(st, "ones", [128, 128], BF16)
        self.blk1 = k.buf(st, "blk1", [128, 128], BF16)
        self.onesf = k.buf(st, "onesf", [128, 128], F32)
        self.modv = k.buf(st, "modv", [128, 2, 72, 2], F32)
        self.modA = k.buf(st, "modA", [128, 2, 3, 2, 8], F32)
        self.modG = k.buf(st, "modG", [128, 2, 3, 2, 8], F32)
        self.epsb = k.buf(st, "epsb", [128, 1], F32)
        self.lruw = k.buf(st, "lruw", [128, 16, 128], BF16, "const")
        self.sp8 = k.buf(st, "sp8", [128, 8], F32)
        self.nlam = k.buf(st, "nlam", [128, 1], F32)
        self.esink = k.buf(st, "esink", [128, 8], F32)
        self.qge = k.buf(st, "qge", [128, 4], F32)
        self.sgl = k.buf(st, "sgl", [128, 1], F32)
        self.pst = k.psum(st, "psall", [128, 8, 512], F32)
        self.psb = [Buf(f"ps{i}", (self.pst[:, i, :] if not k.dry else Dummy())) for i in range(8)]
        ws.add_ring("wfm", [k.buf(st, f"wfm_{i}", [128, 8, 128], BF16, f"wfm_{i}") for i in range(3)])
        ws.add_ring("wv", [k.buf(st, f"wv_{i}", [128, 8, 512], BF16, f"wv_{i}") for i in range(1)])

        k.dma("sp", self.vt.v, dr["vecs"])
        k.dma("pool", self.lruw.v, dr["lruw"])
        self.memset(self.ones.v, 1.0)
        self.memset(self.onesf.v, 1.0)
        self.memset(self.blk1.v, 0.0)
        self.memset(self.blk1[0:64, 0:64], 1.0)
        self.memset(self.blk1[64:128, 64:128], 1.0)
        self.memset(self.epsb.v, EPS)
        self.mod_todo = 0
        self.mod_pending = []
        self.silu_c = k.buf(st, "silu_c", [128, 8, 2], BF16)
        self.compute_small()

        phases = [dict(name="S", T=TS, L=TS, nseq=1, tok0=0, latent=True),
                  dict(name="P", T=TP, L=256, nseq=4, tok0=TS, latent=False)]
        for ph in phases:
            with ExitStack() as pst:
                self.run_phase(pst, ph)
            k.barrier()

    def memset(self, v, val):
        k = self.k
        if k.dry:
            return
        k._sync("dve", [], [v.buf])
        k.cnt["dve"] += 1
        sem = k.sem["dve"]
        ap = v.ap
        k.prog["dve"].append(lambda eng, ap=ap, val=val, sem=sem: eng.memset(ap, val).then_inc(sem, 1))
        k._post((sem, k.cnt["dve"]), [], [v.buf])

    def mod_slot(self, l, sl):
        k, ws, dr = self.k, self.ws, self.dr
        ps = self.psb[7]
        slot = ws.get("wmod", dr["wmod"][l, sl])
        for jl in range(2):
            jc = sl * 2 + jl
            col = l * 144 + jc * 2
            for dc in range(8):
                k.op("pe", "matmul", out=ps[:, col:col + 2], lhsT=slot[:, dc, jl * 128:(jl + 1) * 128],
                     rhs=self.silu_c[:, dc, :], start=(dc == 0), stop=(dc == 7))

    def mod_finish(self, l, i_list, jc0, jc1):
        k, vt = self.k, self.vt
        ps = self.psb[7]
        n = jc1 - jc0
        bm = V(vslice(vt.base, "bmod", l * 72 + jc0, n) if not k.dry else Dummy(), vt)
        for r in range(2):
            src = V(ps.base[:, l * 144 + jc0 * 2:l * 144 + jc1 * 2].rearrange("p (j r) -> p j r", r=2)[:, :, r] if not k.dry else Dummy(), ps)
            k.op("dve", "tensor_tensor", out=self.modv[:, l, jc0:jc1, r], in0=src, in1=bm, op=ALU.add)
        for i in i_list:
            for r in range(2):
                ng = V(vslice(vt.base, "norm_g", (l * 3 + i) * 8, 8) if not k.dry else Dummy(), vt)
                scale = self.modv[:, l, (i * 3 + 1) * 8:(i * 3 + 2) * 8, r]
                gate = self.modv[:, l, (i * 3 + 2) * 8:(i * 3 + 3) * 8, r]
                k.op("dve", "scalar_tensor_tensor", out=self.modA[:, l, i, r, :], in0=scale, scalar=1.0, in1=ng,
                     op0=ALU.add, op1=ALU.mult)
                k.op("dve", "tensor_scalar", out=self.modG[:, l, i, r, :], in0=gate, scalar1=(1.0 if i == 1 else 0.5),
                     scalar2=None, op0=ALU.mult)

    def mod_begin(self, st, which):
        k, ws, vt = self.k, self.ws, self.vt
        ws.add_ring("wmod", [k.buf(st, f"wmod_{i}", [128, 8, 256], BF16, f"wmod_{i}") for i in range(2)])
        pend = self.mod_pending
        if which == 0:
            k.op("act", "activation", out=self.silu_c.v,
                 in_=V(vslice(vt.base, "cond").rearrange("p (c r) -> p c r", r=2) if not k.dry else Dummy(), vt), func=AF.Silu)
            for sl in range(12):
                self.mod_slot(0, sl)
            self.mod_finish(0, [0], 0, 24)
            for sl in range(12, 36):
                pend.append(lambda sl=sl: self.mod_slot(0, sl))
            pend.append(lambda: self.mod_finish(0, [1, 2], 24, 72))
        else:
            for sl in range(36):
                pend.append(lambda sl=sl: self.mod_slot(1, sl))
            pend.append(lambda: self.mod_finish(1, [0, 1, 2], 0, 72))

    def compute_small(self):
        k, vt = self.k, self.vt
        if k.dry:
            return
        vb = vt.base
        lam = V(vslice(vb, "lru_lam"), vt)
        k.op("act", "activation", out=self.sp8.v, in_=lam, func=AF.Exp, scale=-1.0)
        k.op("act", "activation", out=self.sp8.v, in_=self.sp8.v, func=AF.Ln, bias=1.0)
        k.op("dve", "tensor_scalar", out=self.sp8.v, in0=self.sp8.v, scalar1=-8.0, scalar2=None, op0=ALU.mult)
        k.op("act", "activation", out=self.esink.v, in_=V(vslice(vb, "sink"), vt), func=AF.Exp)
        k.op("dve", "tensor_scalar", out=self.qge[:, 0:1], in0=V(vslice(vb, "e_q_g"), vt), scalar1=0.125, scalar2=None, op0=ALU.mult)
        k.op("dve", "tensor_copy", out=self.qge[:, 1:2], in_=V(vslice(vb, "e_k_g"), vt))
        k.op("dve", "tensor_scalar", out=self.qge[:, 2:3], in0=V(vslice(vb, "o_q_g"), vt), scalar1=0.125, scalar2=None, op0=ALU.mult)
        k.op("dve", "tensor_copy", out=self.qge[:, 3:4], in_=V(vslice(vb, "o_k_g"), vt))
        lam_init = 0.8 - 0.6 * math.exp(-0.3 * 0)
        k.op("dve", "tensor_scalar", out=self.sgl.v, in0=V(vslice(vb, "subln_g"), vt), scalar1=(1.0 - lam_init), scalar2=None, op0=ALU.mult)
        with ExitStack() as cst:
            pr = k.buf(cst, "lamprod", [128, 2], F32)
            onesf = k.buf(cst, "onesf", [128, 128], F32)
            ex = k.buf(cst, "lamex", [128, 2], F32)
            self.memset(onesf.v, 1.0)
            self.memset(pr.v, 0.0)
            el = vslice(vb, "e_lam")
            k.op("dve", "tensor_tensor", out=pr[0:64, 0:1], in0=V(el[0:64, 0:1], vt), in1=V(el[0:64, 1:2], vt), op=ALU.mult)
            k.op("dve", "tensor_tensor", out=pr[0:64, 1:2], in0=V(el[0:64, 2:3], vt), in1=V(el[0:64, 3:4], vt), op=ALU.mult)
            ps = self.psb[7]
            k.op("pe", "matmul", out=ps[:, 0:2], lhsT=onesf.v, rhs=pr.v, start=True, stop=True)
            k.op("act", "activation", out=ex.v, in_=ps[:, 0:2], func=AF.Exp)
            k.op("dve", "scalar_tensor_tensor", out=self.nlam.v, in0=ex[:, 1:2], scalar=-lam_init, in1=ex[:, 0:1],
                 op0=ALU.add, op1=ALU.subtract)
            k.barrier()

    def run_phase(self, pst, ph):
        k, dr = self.k, self.dr
        T, tok0 = ph["T"], ph["tok0"]
        NB = T // 512
        self.ph = ph
        self.row = 0 if ph["latent"] else 1
        xt = k.sbuf(pst, "x_" + ph["name"], [128, 8, T], F32)
        xd = k.new_dsem("x")
        self.xb = [[Buf(f"x{dc}_{b}", xt[:, dc, b * 512:(b + 1) * 512] if not k.dry else Dummy(), xd) for b in range(NB)] for dc in range(8)]
        for dc in range(8):
            for b in range(NB):
                k.dma("sp", self.xb[dc][b].v, dr["xin"][dc, :, tok0 + b * 512: tok0 + (b + 1) * 512])
        def ffn_stage(calls, final=False):
            with ExitStack() as fst:
                self.ffn(fst, calls, final)
            k.barrier()
            self.ws.rings.pop("w13")
            self.ws.rings.pop("w2")
        ffn_stage([(0, 0)])
        self.even_mixer(0)
        ffn_stage([(0, 1), (1, 0)])
        self.odd_mixer(1)
        ffn_stage([(1, 1)], final=True)

    def dump(self, i):
        k, dr, ph = self.k, self.dr, self.ph
        for dc in range(8):
            for b in range(ph["T"] // 512):
                k.dma("sp", dr["dbg"][i, dc, :, ph["tok0"] + b * 512: ph["tok0"] + (b + 1) * 512], self.xb[dc][b].v)

    def alloc_norm(self, st):
        k = self.k
        self.sq = [k.buf(st, f"sq{dc}", [128, 512], BF16) for dc in range(8)]
        self.sd = k.buf(st, "sd", [128, 512], F32)
        self.rstd = [k.buf(st, f"rstd{i}", [128, 512], F32) for i in range(2)]
        self.ntmp = [k.buf(st, f"ntmp{i}", [128, 512], F32) for i in range(3)]
        self.nctr = 0

    def norm_block(self, l, i, blk, outs, ps):
        k = self.k
        row = self.row
        xs = [self.xb[dc][blk].v for dc in range(8)]
        for dc in range(8):
            if dc % 2 == 0:
                k.op("act", "activation", out=self.sq[dc].v, in_=xs[dc], func=AF.Square)
            else:
                k.op("pool", "tensor_tensor", out=self.sq[dc].v, in0=xs[dc], in1=xs[dc], op=ALU.mult)
        for dc in range(8):
            k.op("pe", "matmul", out=ps.v, lhsT=self.ones.v, rhs=self.sq[dc].v, start=(dc == 0), stop=(dc == 7))
        k.op("act", "activation", out=self.sd.v, in_=ps.v, func=AF.Ln, scale=1.0 / D, bias=self.epsb.v)
        rstd = self.rstd[self.nctr % 2]
        k.op("act", "activation", out=rstd.v, in_=self.sd.v, func=AF.Exp, scale=-0.5)
        sh0 = (i * 3) * 8
        for dc in range(8):
            tmp = self.ntmp[(self.nctr * 8 + dc) % 3]
            k.op("dve", "tensor_tensor", out=tmp.v, in0=xs[dc], in1=rstd.v, op=ALU.mult)
            k.op("act", "activation", out=outs[dc], in_=tmp.v, func=AF.Identity,
                 scale=self.modA[:, l, i, row, dc:dc + 1], bias=self.modv[:, l, sh0 + dc:sh0 + dc + 1, row])
        self.nctr += 1

    def ffn(self, st, calls, final=False):
        k, ws, dr, ph = self.k, self.ws, self.dr, self.ph
        T = ph["T"]
        row = self.row
        self.alloc_norm(st)
        ws.add_ring("w13", [k.buf(st, f"w13_{i}", [128, 2, 8, 128], BF16, f"w13_{i}") for i in range(3)])
        ws.add_ring("w2", [k.buf(st, f"w2_{i}", [128, NFC, 128], BF16, f"w2_{i}") for i in range(2)])
        xn_t = k.sbuf(st, "xn", [128, 8, 1024], BF16)
        xnb = [[Buf(f"xn{dc}_{s}", xn_t[:, dc, s * 512:(s + 1) * 512] if not k.dry else Dummy()) for s in range(2)] for dc in range(8)]
        g_t = k.sbuf(st, "g", [128, NFC, 1024], BF16)
        gb = [[Buf(f"g{fc}_{s}", g_t[:, fc, s * 512:(s + 1) * 512] if not k.dry else Dummy()) for s in range(2)] for fc in range(NFC)]
        sl = [k.buf(st, f"silu{j}", [128, 512], F32) for j in range(2)]
        psb = self.psb
        items = [(l, a, grp) for (l, a) in calls for grp in range(T // 1024)]
        first = self.mod_todo < 2
        if first:
            self.mod_begin(st, self.mod_todo)
            self.mod_todo += 1

        def do_norm(item):
            l, a, grp = item
            for s in range(2):
                self.norm_block(l, 0 if a == 0 else 2, grp * 2 + s, [xnb[dc][s].v for dc in range(8)], psb[6])
        normed = set()
        ctr = 0
        yctr = 0
        for j, item in enumerate(items):
            l, a, grp = item
            i = 0 if a == 0 else 2
            if j not in normed:
                do_norm(item)
            for fc in range(NFC):
                w = ws.get("w13", dr["w13"][l, a, fc])
                for s in range(2):
                    p1, p3 = psb[(ctr % 2) * 2], psb[(ctr % 2) * 2 + 1]
                    for jj, pp in ((0, p1), (1, p3)):
                        for dc in range(8):
                            k.op("pe", "matmul", out=pp.v, lhsT=w[:, jj, dc, :], rhs=xnb[dc][s].v, start=(dc == 0), stop=(dc == 7))
                    k.op("act", "activation", out=sl[ctr % 2].v, in_=p1.v, func=AF.Silu)
                    k.op("dve", "tensor_tensor", out=gb[fc][s].v, in0=sl[ctr % 2].v, in1=p3.v, op=ALU.mult)
                    ctr += 1
                if self.mod_pending:
                    self.mod_pending.pop(0)()
            if j + 1 < len(items) and items[j + 1][2] != grp:
                do_norm(items[j + 1])
                normed.add(j + 1)
            for dc in range(8):
                w = ws.get("w2", dr["w2"][l, a, dc])
                for s in range(2):
                    pp = psb[4 + yctr % 2]
                    for fc in range(NFC):
                        k.op("pe", "matmul", out=pp.v, lhsT=w[:, fc, :], rhs=gb[fc][s].v, start=(fc == 0), stop=(fc == NFC - 1))
                    xv = self.xb[dc][grp * 2 + s].v
                    k.op("dve", "scalar_tensor_tensor", out=xv, in0=pp.v, scalar=self.modG[:, l, i, row, dc:dc + 1], in1=xv,
                         op0=ALU.mult, op1=ALU.add)
                    yctr += 1
                    if final and j == len(items) - 1 or (final and items[j][:2] == calls[-1]):
                        b_ = grp * 2 + s
                        t0_ = ph["tok0"] + b_ * 512
                        k.dma("sp", dr["yout"][dc, :, t0_:t0_ + 512], xv)
        if first:
            while self.mod_pending:
                self.mod_pending.pop(0)()
            ws.rings.pop("wmod")

    def pieces(self, blk):
        L = self.ph["L"]
        if L >= 512:
            t0 = blk * 512
            return [(t0 // L, t0 % L, 0, 512)]
        per = 512 // L
        return [(blk * per + j, 0, j * L, L) for j in range(per)]

    def vv(self, name, lo=0, n=None):
        if self.k.dry:
            return V(Dummy(), self.vt)
        return V(vslice(self.vt.base, name, lo, n), self.vt)

    def grid(self, t, n1, nblk, width=512):
        k = self.k
        return [[Buf(f"g{a}_{b}", (t[:, a, b * width:(b + 1) * width] if not k.dry else Dummy())) for b in range(nblk)] for a in range(n1)]

    def out_proj(self, l, wname, mix):
        k, ws, dr = self.k, self.ws, self.dr
        NB = self.ph["T"] // 512
        ctr = 0
        for dc in range(8):
            w = ws.get("wfm", dr[wname][dc])
            for blk in range(NB):
                pp = self.psb[ctr % 2]
                ctr += 1
                for kc in range(8):
                    k.op("pe", "matmul", out=pp.v, lhsT=w[:, kc, :], rhs=mix[kc][blk].v, start=(kc == 0), stop=(kc == 7))
                xv = self.xb[dc][blk].v
                k.op("dve", "scalar_tensor_tensor", out=xv, in0=pp.v, scalar=self.modG[:, l, 1, self.row, dc:dc + 1], in1=xv,
                     op0=ALU.mult, op1=ALU.add)

    def qk_inproj(self, st, l, wname, wbase, nchunk, gains, dsts, outs32, hnb, blk, rope, hook=None):
        k, ws, dr = self.k, self.ws, self.dr
        psb = self.psb
        base = self.qctr
        self.qctr += nchunk
        cs = slice(blk * 512, (blk + 1) * 512)

        def A(c):
            g = base + c
            w = ws.get("wfm", dr[wname][wbase + c])
            pa = psb[g % 3]
            for dc in range(8):
                k.op("pe", "matmul", out=pa.v, lhsT=w[:, dc, :], rhs=hnb[dc][0].v, start=(dc == 0), stop=(dc == 7))
            k.op("act", "activation", out=self.qsq[g % 3].v, in_=pa.v, func=AF.Square)

        def B(c):
            g = base + c
            pa, pb_ = psb[g % 3], psb[3 + g % 2]
            k.op("pe", "matmul", out=pb_.v, lhsT=self.blk1.v, rhs=self.qsq[g % 3].v, start=True, stop=True)
            sd = self.qsd[g % 2]
            k.op("act", "activation", out=sd.v, in_=pb_.v, func=AF.Ln, scale=1.0 / 64, bias=self.epsb.v)
            k.op("act", "activation", out=sd.v, in_=sd.v, func=AF.Exp, scale=-0.5)
            dst = dsts[c][blk]
            if not rope and outs32[c] is None:
                k.op("dve", "scalar_tensor_tensor", out=dst.v, in0=pa.v, scalar=gains[c], in1=sd.v, op0=ALU.mult, op1=ALU.mult)
                return
            qn = self.qn[g % 2]
            qo = V(qn.base.bitcast(F32R) if (rope and not k.dry) else qn.base, qn)
            k.op("dve", "scalar_tensor_tensor", out=qo, in0=pa.v, scalar=gains[c], in1=sd.v, op0=ALU.mult, op1=ALU.mult)
            if not rope:
                outs32[c](blk, qn)
                k.op("act", "copy", out=dst.v, in_=qn.v)

        def C(c):
            if not rope:
                return
            g = base + c
            qn = self.qn[g % 2]
            pc = psb[5]
            k.op("pe", "matmul", out=pc.v, lhsT=V(self.rmat.base.bitcast(F32R) if not k.dry else Dummy(), self.rmat),
                 rhs=V(qn.base.bitcast(F32R) if not k.dry else Dummy(), qn), start=True, stop=True)
            t1 = self.qt1[g % 2]
            t2 = self.qt2[g % 2]
            k.op("dve", "tensor_tensor", out=t1.v, in0=qn.v, in1=self.ropec[:, cs], op=ALU.mult)
            k.op("dve", "tensor_tensor", out=t2.v, in0=pc.v, in1=self.ropes[:, cs], op=ALU.mult)
            k.op("dve", "tensor_tensor", out=dsts[c][blk].v, in0=t1.v, in1=t2.v, op=ALU.add)

        for i in range(nchunk + 2):
            if i < nchunk:
                A(i)
                if i == nchunk - 1 and hook is not None:
                    hook()
            if 0 <= i - 1 < nchunk:
                B(i - 1)
            if 0 <= i - 2 < nchunk:
                C(i - 2)

    def alloc_qk(self, st, rope):
        k, dr = self.k, self.dr
        self.qctr = 0
        self.qsq = [k.buf(st, f"qsq{i}", [128, 512], BF16) for i in range(3)]
        self.qsd = [k.buf(st, f"qsd{i}", [128, 512], F32) for i in range(2)]
        self.qn = [k.buf(st, f"qn{i}", [128, 512], F32) for i in range(2)]
        if rope:
            self.qt1 = [k.buf(st, f"qt1{i}", [128, 512], F32) for i in range(2)]
            self.qt2 = [k.buf(st, f"qt2{i}", [128, 512], F32) for i in range(2)]
            self.ropec = k.buf(st, "ropec", [128, TS], BF16, "rope")
            self.ropes = k.buf(st, "ropes", [128, TS], BF16, "rope")
            self.rmat = k.buf(st, "rmat", [128, 128], F32, "rope")
            k.dma("pool", self.ropec.v, dr["ropec"])
            k.dma("pool", self.ropes.v, dr["ropes"])
            self.rmat0 = k.buf(st, "rmat0", [128, 128], F32, "rope")
            k.dma("sp", self.rmat0.v, dr["rmat"])
            k.op("dve", "tensor_copy", out=V(self.rmat.base.bitcast(F32R) if not k.dry else Dummy(), self.rmat), in_=self.rmat0.v)

    def v_inproj(self, wname, width, hnb, blk, vdst, out32):
        k, ws, dr = self.k, self.ws, self.dr
        T = self.ph["T"]
        w = ws.get("wv", dr[wname], (slice(None), slice(None), slice(0, width)))
        for tt in range(4):
            tok = blk * 512 + tt * 128
            pp = self.psb[6 + tt % 2]
            for dc in range(8):
                k.op("pe", "matmul", out=pp[:, 0:width], lhsT=hnb[dc][0][:, tt * 128:(tt + 1) * 128],
                     rhs=w[:, dc, 0:width], start=(dc == 0), stop=(dc == 7))
            ti = tok // 128
            if out32 is not None:
                stg = self.vstg[tt % 2]
                k.op("act", "copy", out=stg[:, 0:width], in_=pp[:, 0:width])
                out32(ti, stg[:, 0:width])
                k.op("dve", "tensor_copy", out=vdst(ti), in_=stg[:, 0:width])
            else:
                k.op("act", "copy", out=vdst(ti), in_=pp[:, 0:width])

    def even_mixer(self, l):
        k, ws, dr, ph = self.k, self.ws, self.dr, self.ph
        T, L, nseq, latent = ph["T"], ph["L"], ph["nseq"], ph["latent"]
        NB = T // 512
        with ExitStack() as mst, ExitStack() as kvst:
            q_t = k.sbuf(mst, "qT", [128, 4, T], BF16)
            qb = self.grid(q_t, 4, NB)
            ctxk = 512 if latent else 0
            NK = ctxk + T
            k_t = k.sbuf(kvst, "kT", [128, 4, NK], BF16)
            kd = k.new_dsem("kv")
            kb = [[Buf(f"k{h}_{b}", (k_t[:, h, b * 512:(b + 1) * 512] if not k.dry else Dummy()), kd) for b in range(NK // 512)] for h in range(4)]
            v_t = k.sbuf(kvst, "vtok", [128, NK // 128, 512], BF16)
            vb = [Buf(f"v{i}", (v_t[:, i, :] if not k.dry else Dummy()), kd) for i in range(NK // 128)]
            if latent:
                for h in range(4):
                    k.dma("pool", kb[h][0].v, dr["cdk"][:, h, :])
                for i in range(4):
                    k.dma("pool", vb[i].v, dr["cdv"][:, i, :])
            with ExitStack() as ast:
                self.alloc_norm(ast)
                self.alloc_qk(ast, latent)
                hn_t = k.sbuf(ast, "hn", [128, 8, 512], BF16)
                hnb = self.grid(hn_t, 8, 1)
                od = k.new_dsem("outk")
                self.vstg = [k.buf(ast, f"vstg{i}", [128, 512], F32, "outk") for i in range(2)]
                for q_ in self.qn:
                    q_.dsem = od
                g_q = self.qge[:, 0:1]
                g_k = self.qge[:, 1:2]
                kb0 = 1 if latent else 0

                def kout(h):
                    if latent:
                        return None
                    return lambda blk, src: k.dma("sp", dr["ndk"][:, h, blk * 512:(blk + 1) * 512], src.v)
                vt0 = 4 if latent else 0
                self.norm_block(l, 1, 0, [hnb[dc][0].v for dc in range(8)], self.psb[6])
                for half in range(NB):
                    def hook(half=half):
                        self.v_inproj("ewv", 512, hnb, half, lambda ti: vb[vt0 + ti].v,
                                      None if latent else (lambda ti, src: k.dma("sp", dr["ndv"][:, ti, :], src)))
                        if half + 1 < NB:
                            self.norm_block(l, 1, half + 1, [hnb[dc][0].v for dc in range(8)], self.psb[6])
                    dsts = [qb[h] for h in range(4)] + [dict((b, kb[h][kb0 + b]) for b in range(NB)) for h in range(4)]
                    self.qk_inproj(ast, l, "ewin", 8, 8, [g_q] * 4 + [g_k] * 4, dsts, [None] * 4 + [kout(h) for h in range(4)], hnb, half, latent,
                                   hook=hook)
            k.barrier()
            with ExitStack() as ast:
                self.diff_attention(ast, qb, kb, vb)
            k.barrier()
            kvst.close()
            y_t = k.sbuf(mst, "ymix", [128, 4, T], BF16)
            yb = self.grid(y_t, 4, NB)
            with ExitStack() as rst:
                self.rnn_branch(rst, l, yb)
            self.out_proj(l, "ewout", [yb[c] for c in range(4)] + [qb[h] for h in range(4)])
        k.barrier()

    def rnn_branch(self, st, l, yb):
        k, ws, dr, ph = self.k, self.ws, self.dr, self.ph
        T, L, nseq, latent = ph["T"], ph["L"], ph["nseq"], ph["latent"]
        NB = T // 512
        psb = self.psb
        xr_t = k.sbuf(st, "xrpad", [128, 4, nseq, L + 4], BF16)
        xrc = [Buf(f"xr{c}", (xr_t[:, c, :, :] if not k.dry else Dummy())) for c in range(4)]
        for c in range(4):
            self.memset(xrc[c].v, 0.0)
        ctr = 0
        with ExitStack() as ist:
            self.alloc_norm(ist)
            hn_t = k.sbuf(ist, "hn", [128, 8, 1024], BF16)
            hnb = self.grid(hn_t, 8, 2)
            for half in range((T + 1023) // 1024):
                for s_ in range(2):
                    self.norm_block(l, 1, half * 2 + s_, [hnb[dc][s_].v for dc in range(8)], psb[6])
                for c8 in range(8):
                    w = ws.get("wfm", dr["ewin"][c8])
                    for s_ in range(2):
                        blk = half * 2 + s_
                        pp = psb[ctr % 2]
                        ctr += 1
                        for dc in range(8):
                            k.op("pe", "matmul", out=pp.v, lhsT=w[:, dc, :], rhs=hnb[dc][s_].v, start=(dc == 0), stop=(dc == 7))
                        if c8 < 4:
                            for (sq_, lo, cl, n) in self.pieces(blk):
                                k.op("act", "copy", out=xrc[c8][:, sq_, 2 + lo:2 + lo + n], in_=pp[:, cl:cl + n])
                        else:
                            k.op("act", "activation", out=yb[c8 - 4][blk].v, in_=pp.v, func=AF.Gelu_apprx_tanh)
        k.barrier()
        SL = L if latent else nseq * L
        RB = min(SL, 1024)
        nrb = SL // RB
        xcs = [k.buf(st, f"xc{i}", [128, RB], F32) for i in range(nrb)]
        xcbs = [k.buf(st, f"xcb{i}", [128, RB], BF16) for i in range(nrb)]
        rrs = [k.buf(st, f"rr{i}", [128, RB], F32) for i in range(2)]
        iis = [k.buf(st, f"ii{i}", [128, RB], F32) for i in range(2)]
        aas = [k.buf(st, f"aa{i}", [128, RB], F32) for i in range(2)]
        hf = k.buf(st, "hf", [128, SL], F32)
        carry = k.buf(st, "carry", [128, 1], F32)
        one1 = k.buf(st, "one1", [128, 1], F32)
        self.memset(one1.v, 1.0)
        if not latent:
            nst = k.buf(st, "nst", [128, 32], F32, "outk")
        gctr = 0
        itc = 0
        for c in range(4):
            xr = xrc[c]
            for rb in range(nrb):
                t0 = rb * RB
                xc, xcb = xcs[rb], xcbs[rb]

                def xin(tap):
                    if latent:
                        return xr[:, 0, t0 + tap:t0 + tap + RB]
                    return xr[:, :, tap:tap + L]
                xco = xc.v if latent else V((xc.base.rearrange("p (s l) -> p s l", s=nseq) if not k.dry else Dummy()), xc)
                k.op("dve", "tensor_scalar", out=xco, in0=xin(0), scalar1=self.vv("conv_w", c * 4, 1),
                     scalar2=self.vv("conv_b", c, 1), op0=ALU.mult, op1=ALU.add)
                for tap in range(1, 4):
                    k.op("dve", "scalar_tensor_tensor", out=xco, in0=xin(tap),
                         scalar=self.vv("conv_w", c * 4 + tap, 1), in1=xco, op0=ALU.mult, op1=ALU.add)
                k.op("act", "copy", out=xcb.v, in_=xc.v)
            for d_ in range(2):
                rbs = list(range(nrb)) if d_ == 0 else list(range(nrb - 1, -1, -1))
                for bi, rb in enumerate(rbs):
                    t0 = rb * RB
                    xc, xcb = xcs[rb], xcbs[rb]
                    rr, ii, aa = rrs[itc % 2], iis[itc % 2], aas[itc % 2]
                    itc += 1
                    for p0 in range(0, RB, 512):
                        n = min(512, RB - p0)
                        pr, pi = psb[(gctr % 2) * 2], psb[(gctr % 2) * 2 + 1]
                        gctr += 1
                        k.op("pe", "matmul", out=pr[:, :n], lhsT=self.lruw[:, (0 * 2 + d_) * 4 + c, :], rhs=xcb[:, p0:p0 + n], start=True, stop=True)
                        k.op("pe", "matmul", out=pi[:, :n], lhsT=self.lruw[:, (1 * 2 + d_) * 4 + c, :], rhs=xcb[:, p0:p0 + n], start=True, stop=True)
                        k.op("act", "activation", out=rr[:, p0:p0 + n], in_=pr[:, :n], func=AF.Sigmoid, bias=self.vv("lru_ba", d_ * 4 + c, 1))
                        k.op("act", "activation", out=ii[:, p0:p0 + n], in_=pi[:, :n], func=AF.Sigmoid, bias=self.vv("lru_bi", d_ * 4 + c, 1))
                    k.op("act", "activation", out=aa.v, in_=rr.v, func=AF.Exp, scale=self.sp8[:, d_ * 4 + c:d_ * 4 + c + 1])
                    k.op("pool", "tensor_tensor", out=rr.v, in0=aa.v, in1=aa.v, op=ALU.mult)
                    k.op("act", "activation", out=rr.v, in_=rr.v, func=AF.Sqrt, scale=-1.0, bias=one1.v)
                    k.op("pool", "tensor_tensor", out=ii.v, in0=ii.v, in1=xc.v, op=ALU.mult)
                    k.op("dve", "tensor_tensor", out=ii.v, in0=ii.v, in1=rr.v, op=ALU.mult)
                    if not latent:
                        edge = 0 if d_ == 0 else L - 1
                        self.memset(aa[:, edge:RB:L], 0.0)
                    if bi == 0:
                        init = self.vv("h0", d_ * 4 + c, 1) if latent else 0.0
                    else:
                        init = hf[:, t0 - 1:t0] if d_ == 0 else carry.v
                    if d_ == 0:
                        k.op("dve", "tensor_tensor_scan", out=hf[:, t0:t0 + RB], data0=aa.v, data1=ii.v, initial=init, op0=ALU.mult, op1=ALU.add)
                        if not latent:
                            k.op("dve", "tensor_copy", out=nst[:, c:32:8], in_=hf[:, L - 1:SL:L])
                    else:
                        k.op("dve", "tensor_tensor_scan", out=rr[:, ::-1], data0=aa[:, ::-1], data1=ii[:, ::-1], initial=init, op0=ALU.mult, op1=ALU.add)
                        if bi < nrb - 1:
                            k.op("dve", "tensor_copy", out=carry.v, in_=rr[:, 0:1])
                        if not latent:
                            k.op("dve", "tensor_copy", out=nst[:, 4 + c:32:8], in_=rr[:, 0:SL:L])
                        k.op("dve", "tensor_tensor", out=rr.v, in0=rr.v, in1=hf[:, t0:t0 + RB], op=ALU.add)
                        for p0 in range(0, RB, 512):
                            n = min(512, RB - p0)
                            tok = t0 + p0
                            yv = yb[c][tok // 512][:, tok % 512:tok % 512 + n]
                            k.op("dve", "tensor_tensor", out=yv, in0=rr[:, p0:p0 + n], in1=yv, op=ALU.mult)
        if not latent:
            k.dma("sp", dr["nst"], nst.v)

    def diff_attention(self, st, qb, kb, vb):
        k, ph = self.k, self.ph
        T, L, nseq, latent = ph["T"], ph["L"], ph["nseq"], ph["latent"]
        psb = self.psb
        QN = min(L, 512)
        ptp = [k.buf(st, f"ptp{i}", [128, 2, 512], BF16) for i in range(4)]
        accp = [[k.buf(st, f"accp{j}_{par}", [128, 2, 512], F32) for par in range(2)] for j in range(2)]
        accb = [k.buf(st, f"accb{j}", [128, 2, 512], BF16) for j in range(2)]
        osbp = [k.buf(st, f"osbp{j}", [128, 2, 512], F32) for j in range(2)]
        rdp = k.buf(st, "rdp", [128, 2, 512], F32)
        ob = [k.buf(st, f"ob{j}", [128, 512], F32) for j in range(2)]
        osq = [k.buf(st, f"osq{j}", [128, 512], BF16) for j in range(2)]
        osd = [k.buf(st, f"osd{j}", [128, 512], F32) for j in range(2)]
        pending = []
        it = 0
        for sq_ in range(nseq):
            for h in range(4):
                for qo in range(0, L, QN):
                    j = it % 2
                    it += 1
                    tok = sq_ * L + qo
                    qbuf = qb[h][tok // 512]
                    qc = slice(tok % 512, tok % 512 + QN)
                    if latent:
                        chunks = [(kb[h][kc // 4], (kc % 4) * 128, vb[kc]) for kc in range(20)]
                    else:
                        chunks = []
                        for jj in range(L // 128):
                            kt = sq_ * L + jj * 128
                            chunks.append((kb[h][kt // 512], kt % 512, vb[kt // 128]))
                    n = len(chunks)

                    def emit_st(i):
                        kbuf, ko, _ = chunks[i]
                        for m in range(2):
                            k.op("pe", "matmul", out=psb[(i % 2) * 2 + m][:, :QN], lhsT=kbuf[64 * m:64 * m + 64, ko:ko + 128],
                                 rhs=qbuf[64 * m:64 * m + 64, qc], start=True, stop=True)
                    emit_st(0)
                    if n > 1:
                        emit_st(1)
                    for i in range(n):
                        vbuf = chunks[i][2]
                        b0 = (i % 2) * 2
                        PT = ptp[i % 4]
                        k.op("act", "activation", out=PT[:, :, :QN], in_=(self.pst[:, b0:b0 + 2, :QN] if not k.dry else None), func=AF.Exp,
                             _r=[psb[b0], psb[b0 + 1]])
                        if i + 2 < n:
                            emit_st(i + 2)
                        for m in range(2):
                            k.op("pe", "matmul", out=psb[4 + m][:, :QN], lhsT=vbuf[:, h * 128:(h + 1) * 128], rhs=PT[:, m, :QN],
                                 start=(i == 0), stop=(i == n - 1))
                        ac = accp[j][i % 2]
                        if i < 2:
                            k.op("dve", "tensor_copy", out=ac[:, :, :QN], in_=PT[:, :, :QN])
                        else:
                            k.op("dve", "tensor_tensor", out=ac[:, :, :QN], in0=ac[:, :, :QN], in1=PT[:, :, :QN], op=ALU.add)
                        if i == min(1, n - 1):
                            while pending:
                                pending.pop(0)()
                    k.op("act", "copy", out=osbp[j][:, :, :QN], in_=(self.pst[:, 4:6, :QN] if not k.dry else None), _r=[psb[4], psb[5]])

                    def tail2(j=j):
                        k.op("dve", "tensor_tensor", out=accb[j][:, :, :QN], in0=accp[j][0][:, :, :QN], in1=accp[j][1][:, :, :QN], op=ALU.add)
                        for m in range(2):
                            k.op("pe", "matmul", out=psb[6 + m][:, :QN], lhsT=self.ones.v, rhs=accb[j][:, m, :QN], start=True, stop=True)
                        k.op("act", "activation", out=rdp[:, :, :QN], in_=(self.pst[:, 6:8, :QN] if not k.dry else None), func=AF.Ln,
                             _r=[psb[6], psb[7]])
                        k.op("act", "activation", out=rdp[:, :, :QN], in_=rdp[:, :, :QN], func=AF.Exp, scale=-1.0)
                        k.op("dve", "tensor_tensor", out=osbp[j][:, :, :QN], in0=osbp[j][:, :, :QN], in1=rdp[:, :, :QN], op=ALU.mult)
                        k.op("dve", "scalar_tensor_tensor", out=ob[j][:, :QN], in0=osbp[j][:, 1, :QN], scalar=self.nlam.v, in1=osbp[j][:, 0, :QN],
                             op0=ALU.mult, op1=ALU.add)
                        k.op("act", "activation", out=osq[j][:, :QN], in_=ob[j][:, :QN], func=AF.Square)

                    def tail3(j=j, qbuf=qbuf, qc=qc):
                        k.op("pe", "matmul", out=psb[6][:, :QN], lhsT=self.ones.v, rhs=osq[j][:, :QN], start=True, stop=True)
                        k.op("act", "activation", out=osd[j][:, :QN], in_=psb[6][:, :QN], func=AF.Ln, scale=1.0 / 128, bias=self.epsb.v)
                        k.op("act", "activation", out=osd[j][:, :QN], in_=osd[j][:, :QN], func=AF.Exp, scale=-0.5)
                        k.op("dve", "scalar_tensor_tensor", out=qbuf[:, qc], in0=ob[j][:, :QN], scalar=self.sgl.v, in1=osd[j][:, :QN],
                             op0=ALU.mult, op1=ALU.mult)
                    pending.append(tail2)
                    pending.append(tail3)
        while pending:
            pending.pop(0)()

    def odd_mixer(self, l):
        k, ws, dr, ph = self.k, self.ws, self.dr, self.ph
        T, L, nseq, latent = ph["T"], ph["L"], ph["nseq"], ph["latent"]
        NB = T // 512
        psb = self.psb
        with ExitStack() as mst:
            q_t = k.sbuf(mst, "qT", [128, 8, T], BF16)
            qb = self.grid(q_t, 8, NB)
            ctxk = 512 if latent else 0
            NK = ctxk + T
            k_t = k.sbuf(mst, "kT", [128, 4, NK], BF16)
            kd = k.new_dsem("kv")
            kb = [[Buf(f"k{h}_{b}", (k_t[:, h, b * 512:(b + 1) * 512] if not k.dry else Dummy()), kd) for b in range(NK // 512)] for h in range(4)]
            v_t = k.sbuf(mst, "vtok", [128, NK // 128, 256], BF16)
            vb = [Buf(f"v{i}", (v_t[:, i, :] if not k.dry else Dummy()), kd) for i in range(NK // 128)]
            if latent:
                for h in range(4):
                    k.dma("pool", kb[h][0].v, dr["cwk"][:, h, :])
                for i in range(4):
                    k.dma("pool", vb[i].v, dr["cwv"][:, i, :])
            with ExitStack() as ast:
                self.alloc_norm(ast)
                self.alloc_qk(ast, latent)
                hn_t = k.sbuf(ast, "hn", [128, 8, 512], BF16)
                hnb = self.grid(hn_t, 8, 1)
                od = k.new_dsem("outk")
                self.vstg = [k.buf(ast, f"vstg{i}", [128, 256], F32, "outk") for i in range(2)]
                for q_ in self.qn:
                    q_.dsem = od
                g_q = self.qge[:, 2:3]
                g_k = self.qge[:, 3:4]
                kb0 = 1 if latent else 0

                def kout(h):
                    if latent:
                        return None
                    return lambda blk, src: k.dma("sp", dr["nwk"][:, h, blk * 512:(blk + 1) * 512], src[0:64, :])
                vt0 = 4 if latent else 0
                self.norm_block(l, 1, 0, [hnb[dc][0].v for dc in range(8)], psb[6])
                for half in range(NB):
                    def hook(half=half):
                        self.v_inproj("owv", 256, hnb, half, lambda ti: vb[vt0 + ti].v,
                                      None if latent else (lambda ti, src: k.dma("sp", dr["nwv"][:, ti, :], src)))
                        if half + 1 < NB:
                            self.norm_block(l, 1, half + 1, [hnb[dc][0].v for dc in range(8)], psb[6])
                    dsts = [qb[c] for c in range(8)] + [dict((b, kb[h][kb0 + b]) for b in range(NB)) for h in range(4)]
                    self.qk_inproj(ast, l, "owin", 0, 12, [g_q] * 8 + [g_k] * 4, dsts, [None] * 8 + [kout(h) for h in range(4)], hnb, half, latent,
                                   hook=hook)
            k.barrier()
            with ExitStack() as ast:
                self.win_attention(ast, qb, kb, vb)
            self.out_proj(l, "owout", [qb[c] for c in range(8)])
        k.barrier()

    def win_attention(self, st, qb, kb, vb):
        k, dr, ph = self.k, self.dr, self.ph
        T, L, nseq, latent = ph["T"], ph["L"], ph["nseq"], ph["latent"]
        psb = self.psb
        QN = min(L, 512)
        ptp = [k.buf(st, f"ptp{i}", [128, 2, 512], BF16) for i in range(4)]
        ptm = [k.buf(st, f"ptm{i}", [128, 2, 512], BF16) for i in range(2)]
        accp = [[k.buf(st, f"accp{j}_{par}", [128, 2, 512], F32) for par in range(2)] for j in range(2)]
        den = k.buf(st, "den", [128, 512], F32)
        accb = [k.buf(st, f"accb{j}", [128, 2, 512], BF16) for j in range(2)]
        if latent:
            band = k.buf(st, "band", [128, 2, 384], BF16, "band")
            k.dma("pool", band[:, 0, :], dr["band"])
            k.dma("pool", band[:, 1, :], dr["band"])
        pending = []
        it = 0
        for sq_ in range(nseq):
            for cq in range(8):
                kvh = cq // 2
                for qo in range(0, L, QN):
                    j = it % 2
                    it += 1
                    tok = sq_ * L + qo
                    qbuf = qb[cq][tok // 512]
                    q0 = tok % 512
                    chunks = []
                    if latent:
                        for jj in range(4):
                            chunks.append((kb[kvh][0], jj * 128, vb[jj], 0, QN, None))
                        for kbp in range(max(0, qo - 128), min(L, qo + QN + 128), 128):
                            qlo = max(kbp - 128, qo)
                            qhi = min(kbp + 256, qo + QN)
                            kcol = 512 + kbp
                            chunks.append((kb[kvh][kcol // 512], kcol % 512, vb[4 + kbp // 128], qlo - qo, qhi - qlo, qlo - (kbp - 128)))
                    else:
                        for jj in range(L // 128):
                            kt = sq_ * L + jj * 128
                            chunks.append((kb[kvh][kt // 512], kt % 512, vb[kt // 128], 0, QN, None))
                    n = len(chunks)
                    O = psb[4 + j]
                    lag = [None]

                    def emit_st(i):
                        kbuf, ko, _, c0, nn, _ = chunks[i]
                        for hf_ in range(2):
                            ps_ = slice(64 * hf_, 64 * hf_ + 64)
                            k.op("pe", "matmul", out=psb[(i % 2) * 2 + hf_][:, :nn], lhsT=kbuf[ps_, ko:ko + 128],
                                 rhs=qbuf[ps_, q0 + c0:q0 + c0 + nn], start=True, stop=True)
                    emit_st(0)
                    if n > 1:
                        emit_st(1)
                    for i in range(n):
                        kbuf, ko, vbuf, c0, nn, boff = chunks[i]
                        b0 = (i % 2) * 2
                        PT = ptp[i % 4]
                        k.op("act", "activation", out=PT[:, :, :nn], in_=(self.pst[:, b0:b0 + 2, :nn] if not k.dry else None), func=AF.Exp,
                             _r=[psb[b0], psb[b0 + 1]])
                        if i + 2 < n:
                            emit_st(i + 2)
                        if boff is not None:
                            PM = ptm[i % 2]
                            k.op("dve", "tensor_tensor", out=PM[:, :, :nn], in0=PT[:, :, :nn], in1=band[:, :, boff:boff + nn], op=ALU.mult)
                            PT = PM
                        for hf_ in range(2):
                            ps_ = slice(64 * hf_, 64 * hf_ + 64)
                            k.op("pe", "matmul", out=O[ps_, c0:c0 + nn], lhsT=vbuf[:, kvh * 64:(kvh + 1) * 64], rhs=PT[:, hf_, :nn],
                                 start=(i == 0), stop=(i == n - 1))
                        def do_acc(i=i, PT=PT, c0=c0, nn=nn, j=j):
                            ac = accp[j][i % 2]
                            if i < 2:
                                assert c0 == 0 and nn == QN
                                k.op("dve", "tensor_copy", out=ac[:, :, :QN], in_=PT[:, :, :QN])
                            else:
                                k.op("dve", "tensor_tensor", out=ac[:, :, c0:c0 + nn], in0=ac[:, :, c0:c0 + nn], in1=PT[:, :, :nn], op=ALU.add)
                        if lag[0] is not None:
                            lag[0]()
                        lag[0] = do_acc
                        if i == n - 1:
                            lag[0]()
                            lag[0] = None
                        if i == min(1, n - 1):
                            while pending:
                                pending.pop(0)()

                    def tail(j=j, O=O, qbuf=qbuf, q0=q0, cq=cq):
                        k.op("dve", "tensor_tensor", out=accb[j][:, :, :QN], in0=accp[j][0][:, :, :QN], in1=accp[j][1][:, :, :QN], op=ALU.add)
                        for hf_ in range(2):
                            k.op("pe", "matmul", out=psb[6 + hf_][:, :QN], lhsT=self.ones.v, rhs=accb[j][:, hf_, :QN], start=True, stop=True)
                        for hf_ in range(2):
                            ps_ = slice(64 * hf_, 64 * hf_ + 64)
                            k.op("act", "activation", out=den[ps_, :QN], in_=psb[6 + hf_][ps_, :QN], func=AF.Ln, bias=self.esink[ps_, cq:cq + 1])
                        k.op("act", "activation", out=den[:, :QN], in_=den[:, :QN], func=AF.Exp, scale=-1.0)
                        k.op("dve", "tensor_tensor", out=qbuf[:, q0:q0 + QN], in0=O[:, :QN], in1=den[:, :QN], op=ALU.mult)
                    pending.append(tail)
        while pending:
            pending.pop(0)()


def _prep_shared(inp):
    f = lambda a: np.ascontiguousarray(a, dtype=np.float32)
    sh = {}
    wm = inp["w_mod"]
    sh["wmod"] = f(np.stack([wm[l].reshape(8, 128, 36, 256).transpose(2, 1, 0, 3) for l in range(2)]))
    w1, w3, w2 = inp["ffn_w1"], inp["ffn_w3"], inp["ffn_w2"]
    w13 = np.empty((2, 2, NFC, 128, 2, 8, 128), np.float32)
    w2r = np.empty((2, 2, 8, 128, NFC, 128), np.float32)
    for l in range(2):
        for a in range(2):
            w13[l, a, :, :, 0] = w1[l, a].reshape(8, 128, NFC, 128).transpose(2, 1, 0, 3)
            w13[l, a, :, :, 1] = w3[l, a].reshape(8, 128, NFC, 128).transpose(2, 1, 0, 3)
            w2r[l, a] = w2[l, a].reshape(NFC, 128, 8, 128).transpose(2, 1, 0, 3)
    sh["w13"] = w13
    sh["w2"] = w2r
    ew = inp["e_w_in"][0]
    sh["ewin"] = f(ew[:, 0:2048].reshape(8, 128, 16, 128).transpose(2, 1, 0, 3))
    sh["ewv"] = f(ew[:, 2048:2560].reshape(8, 128, 512).transpose(1, 0, 2))
    sh["ewout"] = f(inp["e_w_out"][0].reshape(8, 128, 8, 128).transpose(2, 1, 0, 3))
    lruw = np.zeros((128, 2, 2, 4, 128), np.float32)
    for gi, wsrc in enumerate((inp["e_lru_wa"][0], inp["e_lru_wi"][0])):
        for d in range(2):
            for c in range(4):
                lruw[0:64, gi, d, c, 0:64] = wsrc[d, 2 * c]
                lruw[64:128, gi, d, c, 64:128] = wsrc[d, 2 * c + 1]
    sh["lruw"] = lruw.reshape(128, 16, 128)
    ow = inp["o_w_in"][0]
    qcols = ow[:, 0:1024]
    kcols = ow[:, 1024:1280].reshape(1024, 4, 64)
    kdup = np.concatenate([kcols, kcols], axis=2).reshape(1024, 512)
    fm = np.concatenate([qcols, kdup], axis=1)
    sh["owin"] = f(fm.reshape(8, 128, 12, 128).transpose(2, 1, 0, 3))
    sh["owv"] = f(ow[:, 1280:1536].reshape(8, 128, 256).transpose(1, 0, 2))
    sh["owout"] = f(inp["o_w_out"][0].reshape(8, 128, 8, 128).transpose(2, 1, 0, 3))
    C, S = rope_tables()
    sh["ropec"], sh["ropes"] = C, S
    sh["rmat"] = rot_matrix()
    sh["band"] = band_mask()
    return sh


def _prep_vecs(inp, b):
    v = np.zeros((128, NVEC), np.float32)

    def put(name, arr):
        o, n = VEC[name]
        arr = np.asarray(arr, np.float32).reshape(arr.shape[0], -1)
        assert arr.shape[1] == n, (name, arr.shape, n)
        v[:arr.shape[0], o:o + n] = arr

    put("norm_g", inp["norm_g"].reshape(6, 8, 128).transpose(2, 0, 1))
    put("bmod", inp["b_mod"].reshape(2, 72, 128).transpose(2, 0, 1))
    cond = np.stack([inp["c"][b], inp["c_ctx"]])
    put("cond", cond.reshape(2, 8, 128).transpose(2, 1, 0))
    put("conv_w", inp["e_conv_w"][0].reshape(4, 4, 128).transpose(2, 1, 0))
    put("conv_b", inp["e_conv_b"][0].reshape(4, 128).T)
    put("lru_ba", inp["e_lru_ba"][0].reshape(2, 4, 128).transpose(2, 0, 1))
    put("lru_bi", inp["e_lru_bi"][0].reshape(2, 4, 128).transpose(2, 0, 1))
    put("lru_lam", inp["e_lru_lam"][0].reshape(2, 4, 128).transpose(2, 0, 1))
    put("h0", inp["state_lru"][b, 0].reshape(2, 4, 128).transpose(2, 0, 1))
    put("e_q_g", np.tile(inp["e_q_g"][0], 2)[:, None])
    put("e_k_g", np.tile(inp["e_k_g"][0], 2)[:, None])
    put("subln_g", inp["e_subln_g"][0][:, None])
    put("e_lam", inp["e_lam"][0].T)
    put("o_q_g", np.tile(inp["o_q_g"][0], 2)[:, None])
    put("o_k_g", np.tile(inp["o_k_g"][0], 2)[:, None])
    sk = inp["o_sink"][0].reshape(8, 2)
    put("sink", np.repeat(sk.T, 64, axis=0))
    return v


_PROG = None


def kernel(**inp):
    global _PROG
    inp = {k_: np.asarray(v_) for k_, v_ in inp.items()}
    if _PROG is None:
        p = Prog()
        p.build()
        _PROG = p
    prog = _PROG
    sh = _prep_shared(inp)
    in_maps = []
    f = lambda a: np.ascontiguousarray(a, dtype=np.float32)
    for b in range(8):
        m = dict(sh)
        xs = inp["x_sample"][b].T.reshape(8, 128, TS)
        xp = inp["x_prompt"][4 * b:4 * b + 4].reshape(TP, D).T.reshape(8, 128, TP)
        m["xin"] = f(np.concatenate([xs, xp], axis=2))
        m["vecs"] = _prep_vecs(inp, b)
        m["cdk"] = f(inp["cache_diff_k"][b, 0].transpose(2, 1, 0))
        m["cdv"] = f(inp["cache_diff_v"][b, 0].reshape(4, 128, 512).transpose(1, 0, 2))
        ckw = inp["cache_win_k"][b, 0].transpose(2, 1, 0)
        m["cwk"] = f(np.concatenate([ckw, ckw], axis=0))
        m["cwv"] = f(inp["cache_win_v"][b, 0].reshape(4, 128, 256).transpose(1, 0, 2))
        in_maps.append(m)
    import time as _t, sys as _s
    _t0 = _t.time()
    res = run_bass_kernel_spmd(prog.nc, in_maps, core_ids=list(range(8)))
    print(f"[kernel] launch {_t.time() - _t0:.1f}s", file=_s.stderr)
    R = res.results
    y_prompt = np.empty((32, 256, D), np.float32)
    y_sample = np.empty((8, TS, D), np.float32)
    ndk = np.empty((32, 1, 256, 4, 128), np.float32)
    ndv = np.empty((32, 1, 256, 4, 128), np.float32)
    nst = np.empty((32, 1, 2, 512), np.float32)
    nwk = np.empty((32, 1, 256, 4, 64), np.float32)
    nwv = np.empty((32, 1, 256, 4, 64), np.float32)
    for b in range(8):
        r = R[b]
        yo = r["yout"].reshape(D, TT).T
        y_sample[b] = yo[:TS]
        y_prompt[4 * b:4 * b + 4] = yo[TS:].reshape(4, 256, D)
        ndk[4 * b:4 * b + 4, 0] = r["ndk"].transpose(2, 1, 0).reshape(4, 256, 4, 128)
        ndv[4 * b:4 * b + 4, 0] = r["ndv"].transpose(1, 0, 2).reshape(4, 256, 4, 128)
        nst[4 * b:4 * b + 4, 0] = r["nst"].reshape(128, 4, 2, 4).transpose(1, 2, 3, 0).reshape(4, 2, 512)
        nwk[4 * b:4 * b + 4, 0] = r["nwk"].transpose(2, 1, 0).reshape(4, 256, 4, 64)
        nwv[4 * b:4 * b + 4, 0] = r["nwv"].transpose(1, 0, 2).reshape(4, 256, 4, 64)
    if DEBUG:
        kernel.dbg = [R[b]["dbg"] for b in range(8)]
    return (y_prompt, y_sample, ndk, ndv, nst, nwk, nwv)
```

```python
import math
import os
import numpy as np
import concourse.bass as bass
import concourse.mybir as mybir
from concourse.bass_utils import run_bass_kernel_spmd
from contextlib import ExitStack

F32 = mybir.dt.float32
BF16 = mybir.dt.bfloat16
F32R = mybir.dt.float32r
AF = mybir.ActivationFunctionType
ALU = mybir.AluOpType

D = 1024
DFF = 2816
NFC = 22
EPS = 1e-6
TS, TP = 2048, 1024
TT = TS + TP
DEBUG = bool(int(os.environ.get("MK_DEBUG", "0")))
STAGES = os.environ.get("MK_STAGES", "all")


class DSem:
    __slots__ = ("sem", "count")

    def __init__(self, sem):
        self.sem = sem
        self.count = 0


class Dummy:
    def __getitem__(self, idx):
        return self

    def __getattr__(self, name):
        return self

    def __call__(self, *a, **k):
        return self


class V:
    __slots__ = ("ap", "buf")

    def __init__(self, ap, buf):
        self.ap = ap
        self.buf = buf

    def __getitem__(self, idx):
        return V(self.ap[idx], self.buf)


class Buf:
    __slots__ = ("name", "base", "w", "readers", "dsem")

    def __init__(self, name, base, dsem=None):
        self.name = name
        self.base = base
        self.w = None
        self.readers = {}
        self.dsem = dsem

    def __getitem__(self, idx):
        return V(self.base[idx], self)

    @property
    def v(self):
        return V(self.base, self)


class K:
    ENG = ("pe", "act", "dve", "pool", "sp")

    def __init__(self, nc, stack):
        self.nc = nc
        self.stack = stack
        self.dry = False
        self.sem = {e: stack.enter_context(nc.semaphore("c_" + e)) for e in self.ENG}
        self.cnt = {e: 0 for e in self.ENG}
        self.known = {e: {} for e in self.ENG}
        self.dsems = []
        self.dsem_by_name = {}
        self.nwait = 0
        self.prog = {e: [] for e in self.ENG}
        self.uid = 0

    def new_dsem(self, name):
        if name in self.dsem_by_name:
            return self.dsem_by_name[name]
        d = DSem(self.stack.enter_context(self.nc.semaphore("d_" + name)))
        self.dsems.append(d)
        self.dsem_by_name[name] = d
        return d

    def sbuf(self, st, name, shape, dtype):
        if self.dry:
            return Dummy()
        self.uid += 1
        return st.enter_context(self.nc.sbuf_tensor(f"{name}_{self.uid}", list(shape), dtype))

    def psum(self, st, name, shape, dtype):
        if self.dry:
            return Dummy()
        self.uid += 1
        return st.enter_context(self.nc.psum_tensor(f"{name}_{self.uid}", list(shape), dtype))

    def buf(self, st, name, shape, dtype, dsem=None):
        t = self.sbuf(st, name, shape, dtype)
        return Buf(name, t[:] if not self.dry else t, self.new_dsem(dsem) if dsem else None)

    def _wait(self, e, sem, val):
        kn = self.known[e]
        if kn.get(id(sem), 0) >= val:
            return
        self.prog[e].append(lambda eng, sem=sem, val=val: eng.wait_ge(sem, val))
        kn[id(sem)] = val
        self.nwait += 1

    def _sync(self, e, reads, writes):
        needs = {}

        def add(tok):
            s, v = tok[0], tok[1]
            if len(tok) > 2:
                v = tok[2].count
            cur = needs.get(id(s))
            if cur is None or cur[1] < v:
                needs[id(s)] = (s, v)

        for b in reads:
            if b.w is not None:
                add(b.w)
        for b in writes:
            if b.w is not None:
                add(b.w)
            for tok in b.readers.values():
                add(tok)
        own = self.sem[e]
        for s, v in needs.values():
            if e == "pe" and s is own:
                continue
            self._wait(e, s, v)

    def _post(self, tok, reads, writes):
        s = tok[0]
        for b in reads:
            if b not in writes:
                b.readers[id(s)] = tok
        for b in writes:
            b.w = tok
            b.readers = {}

    def op(self, e, meth, **kw):
        if self.dry:
            return
        reads, writes = list(kw.pop("_r", [])), list(kw.pop("_w", []))
        kw2 = {}
        for key, v in kw.items():
            if isinstance(v, V):
                if key in ("out", "accum_out"):
                    if v.buf not in writes:
                        writes.append(v.buf)
                else:
                    if v.buf not in reads:
                        reads.append(v.buf)
                kw2[key] = v.ap
            else:
                kw2[key] = v
        self._sync(e, reads, writes)
        self.cnt[e] += 1
        sem = self.sem[e]
        self.prog[e].append(lambda eng, meth=meth, kw2=kw2, sem=sem: getattr(eng, meth)(**kw2).then_inc(sem, 1))
        self._post((sem, self.cnt[e]), reads, writes)

    def dma(self, e, out, in_, **kw):
        if self.dry:
            return
        if isinstance(out, V):
            buf = out.buf
            self._sync(e, [], [buf])
            o_, i_ = out.ap, in_
        else:
            buf = in_.buf
            self._sync(e, [buf], [])
            o_, i_ = out, in_.ap
        d = buf.dsem
        assert d is not None, buf.name
        d.count += 16
        self.prog[e].append(lambda eng, o_=o_, i_=i_, kw=kw, sem=d.sem: eng.dma_start(out=o_, in_=i_, **kw).then_inc(sem, 16))
        tok = (d.sem, d.count, d)
        if isinstance(out, V):
            self._post(tok, [], [buf])
        else:
            self._post(tok, [buf], [])

    def barrier(self):
        if self.dry:
            return
        for e in ("pe", "act", "dve", "pool"):
            if self.cnt[e]:
                self._wait("sp", self.sem[e], self.cnt[e])
        for d in self.dsems:
            if d.count:
                self._wait("sp", d.sem, d.count)
        self.cnt["sp"] += 1
        self.prog["sp"].append(lambda eng, sem=self.sem["sp"]: eng.nop().then_inc(sem, 1))
        for e in ("pe", "act", "dve", "pool"):
            self._wait(e, self.sem["sp"], self.cnt["sp"])
        for e in self.ENG:
            kn = self.known[e]
            for e2 in self.ENG:
                kn[id(self.sem[e2])] = self.cnt[e2]
            for d in self.dsems:
                kn[id(d.sem)] = d.count

    def finish(self):
        for e in ("pe", "act", "dve", "pool"):
            if self.cnt[e]:
                self._wait("sp", self.sem[e], self.cnt[e])
        for d in self.dsems:
            if d.count:
                self._wait("sp", d.sem, d.count)

    def emit(self):
        with self.nc.Block() as block:
            @block.sync
            def _(eng):
                for f in self.prog["sp"]:
                    f(eng)

            @block.tensor
            def _(eng):
                for f in self.prog["pe"]:
                    f(eng)

            @block.scalar
            def _(eng):
                for f in self.prog["act"]:
                    f(eng)

            @block.vector
            def _(eng):
                for f in self.prog["dve"]:
                    f(eng)

            @block.gpsimd
            def _(eng):
                for f in self.prog["pool"]:
                    f(eng)


class WS:
    LOOK = 6

    def __init__(self, k):
        self.k = k
        self.plan = []
        self.n = 0
        self.pos = 0
        self.rings = {}
        self.ring_cur = {}
        self.ring_cnt = {}
        self.epoch = {}

    def add_ring(self, name, slots):
        self.rings[name] = slots
        self.epoch[name] = self.epoch.get(name, 0) + 1

    def start_real(self):
        self.n = 0
        self.pos = 0
        self.ring_cur = {r: 0 for r in self.ring_cnt}
        self.ring_cnt = {r: 0 for r in self.ring_cnt}
        self.epoch = {}

    def get(self, ring, src, shape_idx=None):
        k = self.k
        if k.dry:
            seq = self.ring_cnt.get(ring, 0)
            self.ring_cnt[ring] = seq + 1
            self.plan.append((ring, seq, src, shape_idx, self.epoch[ring]))
            return Buf("dummy", Dummy())
        n = self.n
        self.n += 1
        ring_, seq, _, _, _ = self.plan[n]
        assert ring_ == ring
        self.ring_cur[ring] = seq
        while self.pos < len(self.plan) and self.pos <= n + self.LOOK:
            r, s, src_, sidx, ep = self.plan[self.pos]
            if r not in self.rings or self.epoch.get(r) != ep:
                break
            slots = self.rings[r]
            if s >= self.ring_cur[r] + len(slots):
                break
            slot = slots[s % len(slots)]
            dst = slot.v if sidx is None else slot[sidx]
            k.dma("pool", dst, src_, max_dma_last_dim=8192)
            self.pos += 1
        assert self.pos > n
        slots = self.rings[ring]
        return slots[seq % len(slots)]


def rope_tables():
    t = np.arange(TS)
    row = (t // 64).astype(np.float64)
    col = (t % 64).astype(np.float64)
    inv = 10000.0 ** (-np.arange(0, 32, 2, dtype=np.float64) / 32)
    C = np.zeros((128, TS), np.float32)
    S = np.zeros((128, TS), np.float32)
    for p in range(128):
        hd = p % 64
        half = hd // 32
        j = hd % 16
        pos = row if half == 0 else col
        ang = (pos.astype(np.float32) * np.float32(inv[j].astype(np.float32))).astype(np.float32)
        C[p] = np.cos(ang).astype(np.float32)
        S[p] = np.sin(ang).astype(np.float32)
    return C, S


def rot_matrix():
    R = np.zeros((128, 128), np.float32)
    for j in range(128):
        b = (j // 32) * 32
        jj = j % 32
        if jj < 16:
            R[b + jj + 16, j] = -1.0
        else:
            R[b + jj - 16, j] = 1.0
    return R


def band_mask():
    k = np.arange(128)[:, None]
    q = np.arange(-128, 256)[None, :]
    return (np.abs(q - k) <= 128).astype(np.float32)


VEC = {}
_off = 0


def _vadd(name, n):
    global _off
    VEC[name] = (_off, n)
    _off += n


_vadd("norm_g", 2 * 3 * 8)
_vadd("bmod", 2 * 72)
_vadd("cond", 16)
_vadd("conv_w", 16)
_vadd("conv_b", 4)
_vadd("lru_ba", 8)
_vadd("lru_bi", 8)
_vadd("lru_lam", 8)
_vadd("h0", 8)
_vadd("e_q_g", 1)
_vadd("e_k_g", 1)
_vadd("subln_g", 1)
_vadd("e_lam", 4)
_vadd("o_q_g", 1)
_vadd("o_k_g", 1)
_vadd("sink", 8)
NVEC = _off


def vslice(vt, name, lo=0, n=None):
    o, cnt = VEC[name]
    if n is None:
        n = cnt - lo
    return vt[:, o + lo:o + lo + n]


class Prog:
    def __init__(self):
        self.nc = bass.Bass("TRN2", target_bir_lowering=False)
        nc = self.nc
        self.dr = {}

        def din(name, shape):
            self.dr[name] = nc.dram_tensor(name, list(shape), F32, kind="ExternalInput").ap()

        def dout(name, shape):
            self.dr[name] = nc.dram_tensor(name, list(shape), F32, kind="ExternalOutput").ap()

        din("xin", [8, 128, TT])
        din("vecs", [128, NVEC])
        din("wmod", [2, 36, 128, 8, 256])
        din("w13", [2, 2, NFC, 128, 2, 8, 128])
        din("w2", [2, 2, 8, 128, NFC, 128])
        din("ewin", [16, 128, 8, 128])
        din("ewv", [128, 8, 512])
        din("ewout", [8, 128, 8, 128])
        din("lruw", [128, 16, 128])
        din("owin", [12, 128, 8, 128])
        din("owv", [128, 8, 256])
        din("owout", [8, 128, 8, 128])
        din("cdk", [128, 4, 512])
        din("cdv", [128, 4, 512])
        din("cwk", [128, 4, 512])
        din("cwv", [128, 4, 256])
        din("ropec", [128, TS])
        din("ropes", [128, TS])
        din("rmat", [128, 128])
        din("band", [128, 384])
        dout("yout", [8, 128, TT])
        dout("ndk", [128, 4, TP])
        dout("ndv", [128, 8, 512])
        dout("nst", [128, 32])
        dout("nwk", [64, 4, TP])
        dout("nwv", [128, 8, 256])
        if DEBUG:
            dout("dbg", [12, 8, 128, TT])

    def build(self):
        with ExitStack() as st:
            k = K(self.nc, st)
            ws = WS(k)
            self.k, self.ws = k, ws
            k.dry = True
            self.run(st)
            k.dry = False
            ws.start_real()
            self.run(st)
            k.finish()
            k.emit()
            self.stats = (k.nwait, dict(k.cnt), len(ws.plan))
        return self.nc

    def run(self, st):
        k, ws, dr = self.k, self.ws, self.dr
        self.vt = k.buf(st, "vecs", [128, NVEC], F32, "const")
        self.ones = k.buf(st, "ones", [128, 128], BF16)
        self.blk1 = k.buf(st, "blk1", [128, 128], BF16)
        self.onesf = k.buf(st, "onesf", [128, 128], F32)
        self.modv = k.buf(st, "modv", [128, 2, 72, 2], F32)
        self.modA = k.buf(st, "modA", [128, 2, 3, 2, 8], F32)
        self.modG = k.buf(st, "modG", [128, 2, 3, 2, 8], F32)
        self.epsb = k.buf(st, "epsb", [128, 1], F32)
        self.lruw = k.buf(st, "lruw", [128, 16, 128], BF16, "const")
        self.sp8 = k.buf(st, "sp8", [128, 8], F32)
        self.nlam = k.buf(st, "nlam", [128, 1], F32)
        self.esink = k.buf(st, "esink", [128, 8], F32)
        self.qge = k.buf(st, "qge", [128, 4], F32)
        self.sgl = k.buf(st, "sgl", [128, 1], F32)
        self.pst = k.psum(st, "psall", [128, 8, 512], F32)
        self.psb = [Buf(f"ps{i}", (self.pst[:, i, :] if not k.dry else Dummy())) for i in range(8)]
        ws.add_ring("wfm", [k.buf(st, f"wfm_{i}", [128, 8, 128], BF16, f"wfm_{i}") for i in range(3)])
        ws.add_ring("wv", [k.buf(st, f"wv_{i}", [128, 8, 512], BF16, f"wv_{i}") for i in range(1)])

        k.dma("sp", self.vt.v, dr["vecs"])
        k.dma("pool", self.lruw.v, dr["lruw"])
        self.memset(self.ones.v, 1.0)
        self.memset(self.onesf.v, 1.0)
        self.memset(self.blk1.v, 0.0)
        self.memset(self.blk1[0:64, 0:64], 1.0)
        self.memset(self.blk1[64:128, 64:128], 1.0)
        self.memset(self.epsb.v, EPS)
        self.mod_todo = 0
        self.mod_pending = []
        self.silu_c = k.buf(st, "silu_c", [128, 8, 2], BF16)
        self.compute_small()

        phases = [dict(name="S", T=TS, L=TS, nseq=1, tok0=0, latent=True),
                  dict(name="P", T=TP, L=256, nseq=4, tok0=TS, latent=False)]
        for ph in phases:
            with ExitStack() as pst:
                self.run_phase(pst, ph)
            k.barrier()

    def memset(self, v, val):
        k = self.k
        if k.dry:
            return
        k._sync("dve", [], [v.buf])
        k.cnt["dve"] += 1
        sem = k.sem["dve"]
        ap = v.ap
        k.prog["dve"].append(lambda eng, ap=ap, val=val, sem=sem: eng.memset(ap, val).then_inc(sem, 1))
        k._post((sem, k.cnt["dve"]), [], [v.buf])

    def mod_slot(self, l, sl):
        k, ws, dr = self.k, self.ws, self.dr
        ps = self.psb[7]
        slot = ws.get("wmod", dr["wmod"][l, sl])
        for jl in range(2):
            jc = sl * 2 + jl
            col = l * 144 + jc * 2
            for dc in range(8):
                k.op("pe", "matmul", out=ps[:, col:col + 2], lhsT=slot[:, dc, jl * 128:(jl + 1) * 128],
                     rhs=self.silu_c[:, dc, :], start=(dc == 0), stop=(dc == 7))

    def mod_finish(self, l, i_list, jc0, jc1):
        k, vt = self.k, self.vt
        ps = self.psb[7]
        n = jc1 - jc0
        bm = V(vslice(vt.base, "bmod", l * 72 + jc0, n) if not k.dry else Dummy(), vt)
        for r in range(2):
            src = V(ps.base[:, l * 144 + jc0 * 2:l * 144 + jc1 * 2].rearrange("p (j r) -> p j r", r=2)[:, :, r] if not k.dry else Dummy(), ps)
            k.op("dve", "tensor_tensor", out=self.modv[:, l, jc0:jc1, r], in0=src, in1=bm, op=ALU.add)
        for i in i_list:
            for r in range(2):
                ng = V(vslice(vt.base, "norm_g", (l * 3 + i) * 8, 8) if not k.dry else Dummy(), vt)
                scale = self.modv[:, l, (i * 3 + 1) * 8:(i * 3 + 2) * 8, r]
                gate = self.modv[:, l, (i * 3 + 2) * 8:(i * 3 + 3) * 8, r]
                k.op("dve", "scalar_tensor_tensor", out=self.modA[:, l, i, r, :], in0=scale, scalar=1.0, in1=ng,
                     op0=ALU.add, op1=ALU.mult)
                k.op("dve", "tensor_scalar", out=self.modG[:, l, i, r, :], in0=gate, scalar1=(1.0 if i == 1 else 0.5),
                     scalar2=None, op0=ALU.mult)

    def mod_begin(self, st, which):
        k, ws, vt = self.k, self.ws, self.vt
        ws.add_ring("wmod", [k.buf(st, f"wmod_{i}", [128, 8, 256], BF16, f"wmod_{i}") for i in range(2)])
        pend = self.mod_pending
        if which == 0:
            k.op("act", "activation", out=self.silu_c.v,
                 in_=V(vslice(vt.base, "cond").rearrange("p (c r) -> p c r", r=2) if not k.dry else Dummy(), vt), func=AF.Silu)
            for sl in range(12):
                self.mod_slot(0, sl)
            self.mod_finish(0, [0], 0, 24)
            for sl in range(12, 36):
                pend.append(lambda sl=sl: self.mod_slot(0, sl))
            pend.append(lambda: self.mod_finish(0, [1, 2], 24, 72))
        else:
            for sl in range(36):
                pend.append(lambda sl=sl: self.mod_slot(1, sl))
            pend.append(lambda: self.mod_finish(1, [0, 1, 2], 0, 72))

    def compute_small(self):
        k, vt = self.k, self.vt
        if k.dry:
            return
        vb = vt.base
        lam = V(vslice(vb, "lru_lam"), vt)
        k.op("act", "activation", out=self.sp8.v, in_=lam, func=AF.Exp, scale=-1.0)
        k.op("act", "activation", out=self.sp8.v, in_=self.sp8.v, func=AF.Ln, bias=1.0)
        k.op("dve", "tensor_scalar", out=self.sp8.v, in0=self.sp8.v, scalar1=-8.0, scalar2=None, op0=ALU.mult)
        k.op("act", "activation", out=self.esink.v, in_=V(vslice(vb, "sink"), vt), func=AF.Exp)
        k.op("dve", "tensor_scalar", out=self.qge[:, 0:1], in0=V(vslice(vb, "e_q_g"), vt), scalar1=0.125, scalar2=None, op0=ALU.mult)
        k.op("dve", "tensor_copy", out=self.qge[:, 1:2], in_=V(vslice(vb, "e_k_g"), vt))
        k.op("dve", "tensor_scalar", out=self.qge[:, 2:3], in0=V(vslice(vb, "o_q_g"), vt), scalar1=0.125, scalar2=None, op0=ALU.mult)
        k.op("dve", "tensor_copy", out=self.qge[:, 3:4], in_=V(vslice(vb, "o_k_g"), vt))
        lam_init = 0.8 - 0.6 * math.exp(-0.3 * 0)
        k.op("dve", "tensor_scalar", out=self.sgl.v, in0=V(vslice(vb, "subln_g"), vt), scalar1=(1.0 - lam_init), scalar2=None, op0=ALU.mult)
        with ExitStack() as cst:
            pr = k.buf(cst, "lamprod", [128, 2], F32)
            onesf = k.buf(cst, "onesf", [128, 128], F32)
            ex = k.buf(cst, "lamex", [128, 2], F32)
            self.memset(onesf.v, 1.0)
            self.memset(pr.v, 0.0)
            el = vslice(vb, "e_lam")
            k.op("dve", "tensor_tensor", out=pr[0:64, 0:1], in0=V(el[0:64, 0:1], vt), in1=V(el[0:64, 1:2], vt), op=ALU.mult)
            k.op("dve", "tensor_tensor", out=pr[0:64, 1:2], in0=V(el[0:64, 2:3], vt), in1=V(el[0:64, 3:4], vt), op=ALU.mult)
            ps = self.psb[7]
            k.op("pe", "matmul", out=ps[:, 0:2], lhsT=onesf.v, rhs=pr.v, start=True, stop=True)
            k.op("act", "activation", out=ex.v, in_=ps[:, 0:2], func=AF.Exp)
            k.op("dve", "scalar_tensor_tensor", out=self.nlam.v, in0=ex[:, 1:2], scalar=-lam_init, in1=ex[:, 0:1],
                 op0=ALU.add, op1=ALU.subtract)
            k.barrier()

    def run_phase(self, pst, ph):
        k, dr = self.k, self.dr
        T, tok0 = ph["T"], ph["tok0"]
        NB = T // 512
        self.ph = ph
        self.row = 0 if ph["latent"] else 1
        xt = k.sbuf(pst, "x_" + ph["name"], [128, 8, T], F32)
        xd = [k.new_dsem(f"x{b}") for b in range(NB)]
        self.xb = [[Buf(f"x{dc}_{b}", xt[:, dc, b * 512:(b + 1) * 512] if not k.dry else Dummy(), xd[b]) for b in range(NB)] for dc in range(8)]
        for b in range(NB):
            for dc in range(8):
                k.dma("sp", self.xb[dc][b].v, dr["xin"][dc, :, tok0 + b * 512: tok0 + (b + 1) * 512])
        def ffn_stage(calls, final=False):
            with ExitStack() as fst:
                self.ffn(fst, calls, final)
            k.barrier()
            self.ws.rings.pop("w13")
            self.ws.rings.pop("w2")
        ffn_stage([(0, 0)])
        self.even_mixer(0)
        ffn_stage([(0, 1), (1, 0)])
        self.odd_mixer(1)
        ffn_stage([(1, 1)], final=True)

    def dump(self, i):
        k, dr, ph = self.k, self.dr, self.ph
        for dc in range(8):
            for b in range(ph["T"] // 512):
                k.dma("sp", dr["dbg"][i, dc, :, ph["tok0"] + b * 512: ph["tok0"] + (b + 1) * 512], self.xb[dc][b].v)

    def alloc_norm(self, st):
        k = self.k
        self.sq = [k.buf(st, f"sq{dc}", [128, 512], BF16) for dc in range(8)]
        self.sd = k.buf(st, "sd", [128, 512], F32)
        self.rstd = [k.buf(st, f"rstd{i}", [128, 512], F32) for i in range(2)]
        self.ntmp = [k.buf(st, f"ntmp{i}", [128, 512], F32) for i in range(3)]
        self.nctr = 0

    def norm_block(self, l, i, blk, outs, ps):
        k = self.k
        row = self.row
        xs = [self.xb[dc][blk].v for dc in range(8)]
        for dc in range(8):
            if dc % 2 == 0:
                k.op("act", "activation", out=self.sq[dc].v, in_=xs[dc], func=AF.Square)
            else:
                k.op("pool", "tensor_tensor", out=self.sq[dc].v, in0=xs[dc], in1=xs[dc], op=ALU.mult)
        for dc in range(8):
            k.op("pe", "matmul", out=ps.v, lhsT=self.ones.v, rhs=self.sq[dc].v, start=(dc == 0), stop=(dc == 7))
        k.op("act", "activation", out=self.sd.v, in_=ps.v, func=AF.Ln, scale=1.0 / D, bias=self.epsb.v)
        rstd = self.rstd[self.nctr % 2]
        k.op("act", "activation", out=rstd.v, in_=self.sd.v, func=AF.Exp, scale=-0.5)
        sh0 = (i * 3) * 8
        for dc in range(8):
            tmp = self.ntmp[(self.nctr * 8 + dc) % 3]
            k.op("dve", "tensor_tensor", out=tmp.v, in0=xs[dc], in1=rstd.v, op=ALU.mult)
            k.op("act", "activation", out=outs[dc], in_=tmp.v, func=AF.Identity,
                 scale=self.modA[:, l, i, row, dc:dc + 1], bias=self.modv[:, l, sh0 + dc:sh0 + dc + 1, row])
        self.nctr += 1

    def ffn(self, st, calls, final=False):
        k, ws, dr, ph = self.k, self.ws, self.dr, self.ph
        T = ph["T"]
        row = self.row
        self.alloc_norm(st)
        ws.add_ring("w13", [k.buf(st, f"w13_{i}", [128, 2, 8, 128], BF16, f"w13_{i}") for i in range(3)])
        ws.add_ring("w2", [k.buf(st, f"w2_{i}", [128, NFC, 128], BF16, f"w2_{i}") for i in range(2)])
        xn_t = k.sbuf(st, "xn", [128, 8, 1024], BF16)
        xnb = [[Buf(f"xn{dc}_{s}", xn_t[:, dc, s * 512:(s + 1) * 512] if not k.dry else Dummy()) for s in range(2)] for dc in range(8)]
        g_t = k.sbuf(st, "g", [128, NFC, 1024], BF16)
        gb = [[Buf(f"g{fc}_{s}", g_t[:, fc, s * 512:(s + 1) * 512] if not k.dry else Dummy()) for s in range(2)] for fc in range(NFC)]
        sl = [k.buf(st, f"silu{j}", [128, 512], F32) for j in range(2)]
        psb = self.psb
        items = [(l, a, grp) for (l, a) in calls for grp in range(T // 1024)]
        first = self.mod_todo < 2
        if first:
            self.mod_begin(st, self.mod_todo)
            self.mod_todo += 1

        def do_norm(item):
            l, a, grp = item
            for s in range(2):
                self.norm_block(l, 0 if a == 0 else 2, grp * 2 + s, [xnb[dc][s].v for dc in range(8)], psb[6])
        normed = set()
        ctr = 0
        yctr = 0
        for j, item in enumerate(items):
            l, a, grp = item
            i = 0 if a == 0 else 2
            if j not in normed:
                do_norm(item)
            for fc in range(NFC):
                w = ws.get("w13", dr["w13"][l, a, fc])
                for s in range(2):
                    p1, p3 = psb[(ctr % 2) * 2], psb[(ctr % 2) * 2 + 1]
                    for jj, pp in ((0, p1), (1, p3)):
                        for dc in range(8):
                            k.op("pe", "matmul", out=pp.v, lhsT=w[:, jj, dc, :], rhs=xnb[dc][s].v, start=(dc == 0), stop=(dc == 7))
                    k.op("act", "activation", out=sl[ctr % 2].v, in_=p1.v, func=AF.Silu)
                    k.op("dve", "tensor_tensor", out=gb[fc][s].v, in0=sl[ctr % 2].v, in1=p3.v, op=ALU.mult)
                    ctr += 1
                if self.mod_pending:
                    self.mod_pending.pop(0)()
            if j + 1 < len(items) and items[j + 1][2] != grp:
                do_norm(items[j + 1])
                normed.add(j + 1)
            for dc in range(8):
                w = ws.get("w2", dr["w2"][l, a, dc])
                for s in range(2):
                    pp = psb[4 + yctr % 2]
                    for fc in range(NFC):
                        k.op("pe", "matmul", out=pp.v, lhsT=w[:, fc, :], rhs=gb[fc][s].v, start=(fc == 0), stop=(fc == NFC - 1))
                    xv = self.xb[dc][grp * 2 + s].v
                    k.op("dve", "scalar_tensor_tensor", out=xv, in0=pp.v, scalar=self.modG[:, l, i, row, dc:dc + 1], in1=xv,
                         op0=ALU.mult, op1=ALU.add)
                    yctr += 1
                    if final and j == len(items) - 1 or (final and items[j][:2] == calls[-1]):
                        b_ = grp * 2 + s
                        t0_ = ph["tok0"] + b_ * 512
                        k.dma("sp", dr["yout"][dc, :, t0_:t0_ + 512], xv)
        if first:
            while self.mod_pending:
                self.mod_pending.pop(0)()
            ws.rings.pop("wmod")

    def pieces(self, blk):
        L = self.ph["L"]
        if L >= 512:
            t0 = blk * 512
            return [(t0 // L, t0 % L, 0, 512)]
        per = 512 // L
        return [(blk * per + j, 0, j * L, L) for j in range(per)]

    def vv(self, name, lo=0, n=None):
        if self.k.dry:
            return V(Dummy(), self.vt)
        return V(vslice(self.vt.base, name, lo, n), self.vt)

    def grid(self, t, n1, nblk, width=512):
        k = self.k
        return [[Buf(f"g{a}_{b}", (t[:, a, b * width:(b + 1) * width] if not k.dry else Dummy())) for b in range(nblk)] for a in range(n1)]

    def out_proj(self, l, wname, mix):
        k, ws, dr = self.k, self.ws, self.dr
        NB = self.ph["T"] // 512
        ctr = 0
        for dc in range(8):
            w = ws.get("wfm", dr[wname][dc])
            for blk in range(NB):
                pp = self.psb[ctr % 2]
                ctr += 1
                for kc in range(8):
                    k.op("pe", "matmul", out=pp.v, lhsT=w[:, kc, :], rhs=mix[kc][blk].v, start=(kc == 0), stop=(kc == 7))
                xv = self.xb[dc][blk].v
                k.op("dve", "scalar_tensor_tensor", out=xv, in0=pp.v, scalar=self.modG[:, l, 1, self.row, dc:dc + 1], in1=xv,
                     op0=ALU.mult, op1=ALU.add)

    def qk_inproj(self, st, l, wname, wbase, nchunk, gains, dsts, outs32, hnb, blk, rope, hook=None):
        k, ws, dr = self.k, self.ws, self.dr
        psb = self.psb
        base = self.qctr
        self.qctr += nchunk
        cs = slice(blk * 512, (blk + 1) * 512)

        def A(c):
            g = base + c
            w = ws.get("wfm", dr[wname][wbase + c])
            pa = psb[g % 3]
            for dc in range(8):
                k.op("pe", "matmul", out=pa.v, lhsT=w[:, dc, :], rhs=hnb[dc][0].v, start=(dc == 0), stop=(dc == 7))
            k.op("act", "activation", out=self.qsq[g % 3].v, in_=pa.v, func=AF.Square)

        def B(c):
            g = base + c
            pa, pb_ = psb[g % 3], psb[3 + g % 2]
            k.op("pe", "matmul", out=pb_.v, lhsT=self.blk1.v, rhs=self.qsq[g % 3].v, start=True, stop=True)
            sd = self.qsd[g % 2]
            k.op("act", "activation", out=sd.v, in_=pb_.v, func=AF.Ln, scale=1.0 / 64, bias=self.epsb.v)
            k.op("act", "activation", out=sd.v, in_=sd.v, func=AF.Exp, scale=-0.5)
            dst = dsts[c][blk]
            if not rope and outs32[c] is None:
                k.op("dve", "scalar_tensor_tensor", out=dst.v, in0=pa.v, scalar=gains[c], in1=sd.v, op0=ALU.mult, op1=ALU.mult)
                return
            qn = self.qn[g % 2]
            qo = V(qn.base.bitcast(F32R) if (rope and not k.dry) else qn.base, qn)
            k.op("dve", "scalar_tensor_tensor", out=qo, in0=pa.v, scalar=gains[c], in1=sd.v, op0=ALU.mult, op1=ALU.mult)
            if not rope:
                outs32[c](blk, qn)
                k.op("act", "copy", out=dst.v, in_=qn.v)

        def C(c):
            if not rope:
                return
            g = base + c
            qn = self.qn[g % 2]
            pc = psb[5]
            k.op("pe", "matmul", out=pc.v, lhsT=V(self.rmat.base.bitcast(F32R) if not k.dry else Dummy(), self.rmat),
                 rhs=V(qn.base.bitcast(F32R) if not k.dry else Dummy(), qn), start=True, stop=True)
            t1 = self.qt1[g % 2]
            t2 = self.qt2[g % 2]
            k.op("dve", "tensor_tensor", out=t1.v, in0=qn.v, in1=self.ropec[:, cs], op=ALU.mult)
            k.op("dve", "tensor_tensor", out=t2.v, in0=pc.v, in1=self.ropes[:, cs], op=ALU.mult)
            k.op("dve", "tensor_tensor", out=dsts[c][blk].v, in0=t1.v, in1=t2.v, op=ALU.add)

        for i in range(nchunk + 2):
            if i < nchunk:
                A(i)
                if i == nchunk - 1 and hook is not None:
                    hook()
            if 0 <= i - 1 < nchunk:
                B(i - 1)
            if 0 <= i - 2 < nchunk:
                C(i - 2)

    def alloc_qk(self, st, rope):
        k, dr = self.k, self.dr
        self.qctr = 0
        self.qsq = [k.buf(st, f"qsq{i}", [128, 512], BF16) for i in range(3)]
        self.qsd = [k.buf(st, f"qsd{i}", [128, 512], F32) for i in range(2)]
        self.qn = [k.buf(st, f"qn{i}", [128, 512], F32) for i in range(2)]
        if rope:
            self.qt1 = [k.buf(st, f"qt1{i}", [128, 512], F32) for i in range(2)]
            self.qt2 = [k.buf(st, f"qt2{i}", [128, 512], F32) for i in range(2)]
            self.ropec = k.buf(st, "ropec", [128, TS], BF16, "rope")
            self.ropes = k.buf(st, "ropes", [128, TS], BF16, "rope")
            self.rmat = k.buf(st, "rmat", [128, 128], F32, "rope")
            k.dma("pool", self.ropec.v, dr["ropec"])
            k.dma("pool", self.ropes.v, dr["ropes"])
            self.rmat0 = k.buf(st, "rmat0", [128, 128], F32, "rope")
            k.dma("sp", self.rmat0.v, dr["rmat"])
            k.op("dve", "tensor_copy", out=V(self.rmat.base.bitcast(F32R) if not k.dry else Dummy(), self.rmat), in_=self.rmat0.v)

    def v_inproj(self, wname, width, hnb, blk, vdst, out32):
        k, ws, dr = self.k, self.ws, self.dr
        T = self.ph["T"]
        w = ws.get("wv", dr[wname], (slice(None), slice(None), slice(0, width)))
        for tt in range(4):
            tok = blk * 512 + tt * 128
            pp = self.psb[6 + tt % 2]
            for dc in range(8):
                k.op("pe", "matmul", out=pp[:, 0:width], lhsT=hnb[dc][0][:, tt * 128:(tt + 1) * 128],
                     rhs=w[:, dc, 0:width], start=(dc == 0), stop=(dc == 7))
            ti = tok // 128
            if out32 is not None:
                stg = self.vstg[tt % 2]
                k.op("act", "copy", out=stg[:, 0:width], in_=pp[:, 0:width])
                out32(ti, stg[:, 0:width])
                k.op("dve", "tensor_copy", out=vdst(ti), in_=stg[:, 0:width])
            else:
                k.op("act", "copy", out=vdst(ti), in_=pp[:, 0:width])

    def even_mixer(self, l):
        k, ws, dr, ph = self.k, self.ws, self.dr, self.ph
        T, L, nseq, latent = ph["T"], ph["L"], ph["nseq"], ph["latent"]
        NB = T // 512
        with ExitStack() as mst, ExitStack() as kvst:
            q_t = k.sbuf(mst, "qT", [128, 4, T], BF16)
            qb = self.grid(q_t, 4, NB)
            ctxk = 512 if latent else 0
            NK = ctxk + T
            k_t = k.sbuf(kvst, "kT", [128, 4, NK], BF16)
            kd = k.new_dsem("kv")
            kb = [[Buf(f"k{h}_{b}", (k_t[:, h, b * 512:(b + 1) * 512] if not k.dry else Dummy()), kd) for b in range(NK // 512)] for h in range(4)]
            v_t = k.sbuf(kvst, "vtok", [128, NK // 128, 512], BF16)
            vb = [Buf(f"v{i}", (v_t[:, i, :] if not k.dry else Dummy()), kd) for i in range(NK // 128)]
            if latent:
                for h in range(4):
                    k.dma("pool", kb[h][0].v, dr["cdk"][:, h, :])
                for i in range(4):
                    k.dma("pool", vb[i].v, dr["cdv"][:, i, :])
            with ExitStack() as ast:
                self.alloc_norm(ast)
                self.alloc_qk(ast, latent)
                hn_t = k.sbuf(ast, "hn", [128, 8, 512], BF16)
                hnb = self.grid(hn_t, 8, 1)
                od = k.new_dsem("outk")
                self.vstg = [k.buf(ast, f"vstg{i}", [128, 512], F32, "outk") for i in range(2)]
                for q_ in self.qn:
                    q_.dsem = od
                g_q = self.qge[:, 0:1]
                g_k = self.qge[:, 1:2]
                kb0 = 1 if latent else 0

                def kout(h):
                    if latent:
                        return None
                    return lambda blk, src: k.dma("sp", dr["ndk"][:, h, blk * 512:(blk + 1) * 512], src.v)
                vt0 = 4 if latent else 0
                self.norm_block(l, 1, 0, [hnb[dc][0].v for dc in range(8)], self.psb[6])
                for half in range(NB):
                    def hook(half=half):
                        self.v_inproj("ewv", 512, hnb, half, lambda ti: vb[vt0 + ti].v,
                                      None if latent else (lambda ti, src: k.dma("sp", dr["ndv"][:, ti, :], src)))
                        if half + 1 < NB:
                            self.norm_block(l, 1, half + 1, [hnb[dc][0].v for dc in range(8)], self.psb[6])
                    dsts = [qb[h] for h in range(4)] + [dict((b, kb[h][kb0 + b]) for b in range(NB)) for h in range(4)]
                    self.qk_inproj(ast, l, "ewin", 8, 8, [g_q] * 4 + [g_k] * 4, dsts, [None] * 4 + [kout(h) for h in range(4)], hnb, half, latent,
                                   hook=hook)
            k.barrier()
            with ExitStack() as ast:
                self.diff_attention(ast, qb, kb, vb)
            k.barrier()
            kvst.close()
            y_t = k.sbuf(mst, "ymix", [128, 4, T], BF16)
            yb = self.grid(y_t, 4, NB)
            with ExitStack() as rst:
                self.rnn_branch(rst, l, yb)
            self.out_proj(l, "ewout", [yb[c] for c in range(4)] + [qb[h] for h in range(4)])
        k.barrier()

    def rnn_branch(self, st, l, yb):
        k, ws, dr, ph = self.k, self.ws, self.dr, self.ph
        T, L, nseq, latent = ph["T"], ph["L"], ph["nseq"], ph["latent"]
        NB = T // 512
        psb = self.psb
        xr_t = k.sbuf(st, "xrpad", [128, 4, nseq, L + 4], BF16)
        xrc = [Buf(f"xr{c}", (xr_t[:, c, :, :] if not k.dry else Dummy())) for c in range(4)]
        for c in range(4):
            self.memset(xrc[c].v, 0.0)
        ctr = 0
        with ExitStack() as ist:
            self.alloc_norm(ist)
            hn_t = k.sbuf(ist, "hn", [128, 8, 1024], BF16)
            hnb = self.grid(hn_t, 8, 2)
            for half in range((T + 1023) // 1024):
                for s_ in range(2):
                    self.norm_block(l, 1, half * 2 + s_, [hnb[dc][s_].v for dc in range(8)], psb[6])
                for c8 in range(8):
                    w = ws.get("wfm", dr["ewin"][c8])
                    for s_ in range(2):
                        blk = half * 2 + s_
                        pp = psb[ctr % 2]
                        ctr += 1
                        for dc in range(8):
                            k.op("pe", "matmul", out=pp.v, lhsT=w[:, dc, :], rhs=hnb[dc][s_].v, start=(dc == 0), stop=(dc == 7))
                        if c8 < 4:
                            for (sq_, lo, cl, n) in self.pieces(blk):
                                k.op("act", "copy", out=xrc[c8][:, sq_, 2 + lo:2 + lo + n], in_=pp[:, cl:cl + n])
                        else:
                            k.op("act", "activation", out=yb[c8 - 4][blk].v, in_=pp.v, func=AF.Gelu_apprx_tanh)
        k.barrier()
        SL = L if latent else nseq * L
        RB = min(SL, 1024)
        nrb = SL // RB
        xcs = [k.buf(st, f"xc{i}", [128, RB], F32) for i in range(nrb)]
        xcbs = [k.buf(st, f"xcb{i}", [128, RB], BF16) for i in range(nrb)]
        rrs = [k.buf(st, f"rr{i}", [128, RB], F32) for i in range(2)]
        iis = [k.buf(st, f"ii{i}", [128, RB], F32) for i in range(2)]
        aas = [k.buf(st, f"aa{i}", [128, RB], F32) for i in range(2)]
        hf = k.buf(st, "hf", [128, SL], F32)
        carry = k.buf(st, "carry", [128, 1], F32)
        one1 = k.buf(st, "one1", [128, 1], F32)
        self.memset(one1.v, 1.0)
        if not latent:
            nst = k.buf(st, "nst", [128, 32], F32, "outk")
        gctr = 0
        itc = 0
        for c in range(4):
            xr = xrc[c]
            for rb in range(nrb):
                t0 = rb * RB
                xc, xcb = xcs[rb], xcbs[rb]

                def xin(tap):
                    if latent:
                        return xr[:, 0, t0 + tap:t0 + tap + RB]
                    return xr[:, :, tap:tap + L]
                xco = xc.v if latent else V((xc.base.rearrange("p (s l) -> p s l", s=nseq) if not k.dry else Dummy()), xc)
                k.op("dve", "tensor_scalar", out=xco, in0=xin(0), scalar1=self.vv("conv_w", c * 4, 1),
                     scalar2=self.vv("conv_b", c, 1), op0=ALU.mult, op1=ALU.add)
                for tap in range(1, 4):
                    k.op("dve", "scalar_tensor_tensor", out=xco, in0=xin(tap),
                         scalar=self.vv("conv_w", c * 4 + tap, 1), in1=xco, op0=ALU.mult, op1=ALU.add)
                k.op("act", "copy", out=xcb.v, in_=xc.v)
            for d_ in range(2):
                rbs = list(range(nrb)) if d_ == 0 else list(range(nrb - 1, -1, -1))
                for bi, rb in enumerate(rbs):
                    t0 = rb * RB
                    xc, xcb = xcs[rb], xcbs[rb]
                    rr, ii, aa = rrs[itc % 2], iis[itc % 2], aas[itc % 2]
                    itc += 1
                    for p0 in range(0, RB, 512):
                        n = min(512, RB - p0)
                        pr, pi = psb[(gctr % 2) * 2], psb[(gctr % 2) * 2 + 1]
                        gctr += 1
                        k.op("pe", "matmul", out=pr[:, :n], lhsT=self.lruw[:, (0 * 2 + d_) * 4 + c, :], rhs=xcb[:, p0:p0 + n], start=True, stop=True)
                        k.op("pe", "matmul", out=pi[:, :n], lhsT=self.lruw[:, (1 * 2 + d_) * 4 + c, :], rhs=xcb[:, p0:p0 + n], start=True, stop=True)
                        k.op("act", "activation", out=rr[:, p0:p0 + n], in_=pr[:, :n], func=AF.Sigmoid, bias=self.vv("lru_ba", d_ * 4 + c, 1))
                        k.op("act", "activation", out=ii[:, p0:p0 + n], in_=pi[:, :n], func=AF.Sigmoid, bias=self.vv("lru_bi", d_ * 4 + c, 1))
                    k.op("act", "activation", out=aa.v, in_=rr.v, func=AF.Exp, scale=self.sp8[:, d_ * 4 + c:d_ * 4 + c + 1])
                    k.op("pool", "tensor_tensor", out=rr.v, in0=aa.v, in1=aa.v, op=ALU.mult)
                    k.op("act", "activation", out=rr.v, in_=rr.v, func=AF.Sqrt, scale=-1.0, bias=one1.v)
                    k.op("pool", "tensor_tensor", out=ii.v, in0=ii.v, in1=xc.v, op=ALU.mult)
                    k.op("dve", "tensor_tensor", out=ii.v, in0=ii.v, in1=rr.v, op=ALU.mult)
                    if not latent:
                        edge = 0 if d_ == 0 else L - 1
                        self.memset(aa[:, edge:RB:L], 0.0)
                    if bi == 0:
                        init = self.vv("h0", d_ * 4 + c, 1) if latent else 0.0
                    else:
                        init = hf[:, t0 - 1:t0] if d_ == 0 else carry.v
                    if d_ == 0:
                        k.op("dve", "tensor_tensor_scan", out=hf[:, t0:t0 + RB], data0=aa.v, data1=ii.v, initial=init, op0=ALU.mult, op1=ALU.add)
                        if not latent:
                            k.op("dve", "tensor_copy", out=nst[:, c:32:8], in_=hf[:, L - 1:SL:L])
                    else:
                        k.op("dve", "tensor_tensor_scan", out=rr[:, ::-1], data0=aa[:, ::-1], data1=ii[:, ::-1], initial=init, op0=ALU.mult, op1=ALU.add)
                        if bi < nrb - 1:
                            k.op("dve", "tensor_copy", out=carry.v, in_=rr[:, 0:1])
                        if not latent:
                            k.op("dve", "tensor_copy", out=nst[:, 4 + c:32:8], in_=rr[:, 0:SL:L])
                        k.op("dve", "tensor_tensor", out=rr.v, in0=rr.v, in1=hf[:, t0:t0 + RB], op=ALU.add)
                        for p0 in range(0, RB, 512):
                            n = min(512, RB - p0)
                            tok = t0 + p0
                            yv = yb[c][tok // 512][:, tok % 512:tok % 512 + n]
                            k.op("dve", "tensor_tensor", out=yv, in0=rr[:, p0:p0 + n], in1=yv, op=ALU.mult)
        if not latent:
            k.dma("sp", dr["nst"], nst.v)

    def diff_attention(self, st, qb, kb, vb):
        k, ph = self.k, self.ph
        T, L, nseq, latent = ph["T"], ph["L"], ph["nseq"], ph["latent"]
        psb = self.psb
        QN = min(L, 512)
        ptp = [k.buf(st, f"ptp{i}", [128, 2, 512], BF16) for i in range(4)]
        accp = [[k.buf(st, f"accp{j}_{par}", [128, 2, 512], F32) for par in range(2)] for j in range(2)]
        accb = [k.buf(st, f"accb{j}", [128, 2, 512], BF16) for j in range(2)]
        osb = [[k.buf(st, f"osb{m}_{j}", [128, 512], F32) for j in range(2)] for m in range(2)]
        rd = [k.buf(st, f"rd{m}", [128, 512], F32) for m in range(2)]
        ob = [k.buf(st, f"ob{j}", [128, 512], F32) for j in range(2)]
        osq = [k.buf(st, f"osq{j}", [128, 512], BF16) for j in range(2)]
        osd = [k.buf(st, f"osd{j}", [128, 512], F32) for j in range(2)]
        pending = []
        it = 0
        for sq_ in range(nseq):
            for h in range(4):
                for qo in range(0, L, QN):
                    j = it % 2
                    it += 1
                    tok = sq_ * L + qo
                    qbuf = qb[h][tok // 512]
                    qc = slice(tok % 512, tok % 512 + QN)
                    if latent:
                        chunks = [(kb[h][kc // 4], (kc % 4) * 128, vb[kc]) for kc in range(20)]
                    else:
                        chunks = []
                        for jj in range(L // 128):
                            kt = sq_ * L + jj * 128
                            chunks.append((kb[h][kt // 512], kt % 512, vb[kt // 128]))
                    n = len(chunks)

                    def emit_st(i):
                        kbuf, ko, _ = chunks[i]
                        for m in range(2):
                            k.op("pe", "matmul", out=psb[(i % 2) * 2 + m][:, :QN], lhsT=kbuf[64 * m:64 * m + 64, ko:ko + 128],
                                 rhs=qbuf[64 * m:64 * m + 64, qc], start=True, stop=True)
                    emit_st(0)
                    if n > 1:
                        emit_st(1)
                    for i in range(n):
                        vbuf = chunks[i][2]
                        b0 = (i % 2) * 2
                        PT = ptp[i % 4]
                        k.op("act", "activation", out=PT[:, :, :QN], in_=(self.pst[:, b0:b0 + 2, :QN] if not k.dry else None), func=AF.Exp,
                             _r=[psb[b0], psb[b0 + 1]])
                        if i + 2 < n:
                            emit_st(i + 2)
                        for m in range(2):
                            k.op("pe", "matmul", out=psb[4 + m][:, :QN], lhsT=vbuf[:, h * 128:(h + 1) * 128], rhs=PT[:, m, :QN],
                                 start=(i == 0), stop=(i == n - 1))
                        ac = accp[j][i % 2]
                        if i < 2:
                            k.op("dve", "tensor_copy", out=ac[:, :, :QN], in_=PT[:, :, :QN])
                        else:
                            k.op("dve", "tensor_tensor", out=ac[:, :, :QN], in0=ac[:, :, :QN], in1=PT[:, :, :QN], op=ALU.add)
                        if i == min(1, n - 1):
                            while pending:
                                pending.pop(0)()
                    for m in range(2):
                        k.op("act", "copy", out=osb[m][j][:, :QN], in_=psb[4 + m][:, :QN])

                    def tail2(j=j):
                        k.op("dve", "tensor_tensor", out=accb[j][:, :, :QN], in0=accp[j][0][:, :, :QN], in1=accp[j][1][:, :, :QN], op=ALU.add)
                        for m in range(2):
                            k.op("pe", "matmul", out=psb[6 + m][:, :QN], lhsT=self.ones.v, rhs=accb[j][:, m, :QN], start=True, stop=True)
                        for m in range(2):
                            k.op("act", "activation", out=rd[m][:, :QN], in_=psb[6 + m][:, :QN], func=AF.Ln)
                        for m in range(2):
                            k.op("act", "activation", out=rd[m][:, :QN], in_=rd[m][:, :QN], func=AF.Exp, scale=-1.0)
                            k.op("dve", "tensor_tensor", out=osb[m][j][:, :QN], in0=osb[m][j][:, :QN], in1=rd[m][:, :QN], op=ALU.mult)
                        k.op("dve", "scalar_tensor_tensor", out=ob[j][:, :QN], in0=osb[1][j][:, :QN], scalar=self.nlam.v, in1=osb[0][j][:, :QN],
                             op0=ALU.mult, op1=ALU.add)
                        k.op("act", "activation", out=osq[j][:, :QN], in_=ob[j][:, :QN], func=AF.Square)

                    def tail3(j=j, qbuf=qbuf, qc=qc):
                        k.op("pe", "matmul", out=psb[6][:, :QN], lhsT=self.ones.v, rhs=osq[j][:, :QN], start=True, stop=True)
                        k.op("act", "activation", out=osd[j][:, :QN], in_=psb[6][:, :QN], func=AF.Ln, scale=1.0 / 128, bias=self.epsb.v)
                        k.op("act", "activation", out=osd[j][:, :QN], in_=osd[j][:, :QN], func=AF.Exp, scale=-0.5)
                        k.op("dve", "scalar_tensor_tensor", out=qbuf[:, qc], in0=ob[j][:, :QN], scalar=self.sgl.v, in1=osd[j][:, :QN],
                             op0=ALU.mult, op1=ALU.mult)
                    pending.append(tail2)
                    pending.append(tail3)
        while pending:
            pending.pop(0)()

    def odd_mixer(self, l):
        k, ws, dr, ph = self.k, self.ws, self.dr, self.ph
        T, L, nseq, latent = ph["T"], ph["L"], ph["nseq"], ph["latent"]
        NB = T // 512
        psb = self.psb
        with ExitStack() as mst:
            q_t = k.sbuf(mst, "qT", [128, 8, T], BF16)
            qb = self.grid(q_t, 8, NB)
            ctxk = 512 if latent else 0
            NK = ctxk + T
            k_t = k.sbuf(mst, "kT", [128, 4, NK], BF16)
            kd = k.new_dsem("kv")
            kb = [[Buf(f"k{h}_{b}", (k_t[:, h, b * 512:(b + 1) * 512] if not k.dry else Dummy()), kd) for b in range(NK // 512)] for h in range(4)]
            v_t = k.sbuf(mst, "vtok", [128, NK // 128, 256], BF16)
            vb = [Buf(f"v{i}", (v_t[:, i, :] if not k.dry else Dummy()), kd) for i in range(NK // 128)]
            if latent:
                for h in range(4):
                    k.dma("pool", kb[h][0].v, dr["cwk"][:, h, :])
                for i in range(4):
                    k.dma("pool", vb[i].v, dr["cwv"][:, i, :])
            with ExitStack() as ast:
                self.alloc_norm(ast)
                self.alloc_qk(ast, latent)
                hn_t = k.sbuf(ast, "hn", [128, 8, 512], BF16)
                hnb = self.grid(hn_t, 8, 1)
                od = k.new_dsem("outk")
                self.vstg = [k.buf(ast, f"vstg{i}", [128, 256], F32, "outk") for i in range(2)]
                for q_ in self.qn:
                    q_.dsem = od
                g_q = self.qge[:, 2:3]
                g_k = self.qge[:, 3:4]
                kb0 = 1 if latent else 0

                def kout(h):
                    if latent:
                        return None
                    return lambda blk, src: k.dma("sp", dr["nwk"][:, h, blk * 512:(blk + 1) * 512], src[0:64, :])
                vt0 = 4 if latent else 0
                self.norm_block(l, 1, 0, [hnb[dc][0].v for dc in range(8)], psb[6])
                for half in range(NB):
                    def hook(half=half):
                        self.v_inproj("owv", 256, hnb, half, lambda ti: vb[vt0 + ti].v,
                                      None if latent else (lambda ti, src: k.dma("sp", dr["nwv"][:, ti, :], src)))
                        if half + 1 < NB:
                            self.norm_block(l, 1, half + 1, [hnb[dc][0].v for dc in range(8)], psb[6])
                    dsts = [qb[c] for c in range(8)] + [dict((b, kb[h][kb0 + b]) for b in range(NB)) for h in range(4)]
                    self.qk_inproj(ast, l, "owin", 0, 12, [g_q] * 8 + [g_k] * 4, dsts, [None] * 8 + [kout(h) for h in range(4)], hnb, half, latent,
                                   hook=hook)
            k.barrier()
            with ExitStack() as ast:
                self.win_attention(ast, qb, kb, vb)
            self.out_proj(l, "owout", [qb[c] for c in range(8)])
        k.barrier()

    def win_attention(self, st, qb, kb, vb):
        k, dr, ph = self.k, self.dr, self.ph
        T, L, nseq, latent = ph["T"], ph["L"], ph["nseq"], ph["latent"]
        psb = self.psb
        QN = min(L, 512)
        ptp = [k.buf(st, f"ptp{i}", [128, 2, 512], BF16) for i in range(4)]
        ptm = [k.buf(st, f"ptm{i}", [128, 2, 512], BF16) for i in range(2)]
        accp = [[k.buf(st, f"accp{j}_{par}", [128, 2, 512], F32) for par in range(2)] for j in range(2)]
        den = k.buf(st, "den", [128, 512], F32)
        accb = [k.buf(st, f"accb{j}", [128, 2, 512], BF16) for j in range(2)]
        if latent:
            band = k.buf(st, "band", [128, 2, 384], BF16, "band")
            k.dma("pool", band[:, 0, :], dr["band"])
            k.dma("pool", band[:, 1, :], dr["band"])
        pending = []
        it = 0
        for sq_ in range(nseq):
            for cq in range(8):
                kvh = cq // 2
                for qo in range(0, L, QN):
                    j = it % 2
                    it += 1
                    tok = sq_ * L + qo
                    qbuf = qb[cq][tok // 512]
                    q0 = tok % 512
                    chunks = []
                    if latent:
                        for jj in range(4):
                            chunks.append((kb[kvh][0], jj * 128, vb[jj], 0, QN, None))
                        for kbp in range(max(0, qo - 128), min(L, qo + QN + 128), 128):
                            qlo = max(kbp - 128, qo)
                            qhi = min(kbp + 256, qo + QN)
                            kcol = 512 + kbp
                            chunks.append((kb[kvh][kcol // 512], kcol % 512, vb[4 + kbp // 128], qlo - qo, qhi - qlo, qlo - (kbp - 128)))
                    else:
                        for jj in range(L // 128):
                            kt = sq_ * L + jj * 128
                            chunks.append((kb[kvh][kt // 512], kt % 512, vb[kt // 128], 0, QN, None))
                    n = len(chunks)
                    O = psb[4 + j]
                    lag = [None]

                    def emit_st(i):
                        kbuf, ko, _, c0, nn, _ = chunks[i]
                        for hf_ in range(2):
                            ps_ = slice(64 * hf_, 64 * hf_ + 64)
                            k.op("pe", "matmul", out=psb[(i % 2) * 2 + hf_][:, :nn], lhsT=kbuf[ps_, ko:ko + 128],
                                 rhs=qbuf[ps_, q0 + c0:q0 + c0 + nn], start=True, stop=True)
                    emit_st(0)
                    if n > 1:
                        emit_st(1)
                    for i in range(n):
                        kbuf, ko, vbuf, c0, nn, boff = chunks[i]
                        b0 = (i % 2) * 2
                        PT = ptp[i % 4]
                        k.op("act", "activation", out=PT[:, :, :nn], in_=(self.pst[:, b0:b0 + 2, :nn] if not k.dry else None), func=AF.Exp,
                             _r=[psb[b0], psb[b0 + 1]])
                        if i + 2 < n:
                            emit_st(i + 2)
                        if boff is not None:
                            PM = ptm[i % 2]
                            k.op("dve", "tensor_tensor", out=PM[:, :, :nn], in0=PT[:, :, :nn], in1=band[:, :, boff:boff + nn], op=ALU.mult)
                            PT = PM
                        for hf_ in range(2):
                            ps_ = slice(64 * hf_, 64 * hf_ + 64)
                            k.op("pe", "matmul", out=O[ps_, c0:c0 + nn], lhsT=vbuf[:, kvh * 64:(kvh + 1) * 64], rhs=PT[:, hf_, :nn],
                                 start=(i == 0), stop=(i == n - 1))
                        def do_acc(i=i, PT=PT, c0=c0, nn=nn, j=j):
                            ac = accp[j][i % 2]
                            if i < 2:
                                assert c0 == 0 and nn == QN
                                k.op("dve", "tensor_copy", out=ac[:, :, :QN], in_=PT[:, :, :QN])
                            else:
                                k.op("dve", "tensor_tensor", out=ac[:, :, c0:c0 + nn], in0=ac[:, :, c0:c0 + nn], in1=PT[:, :, :nn], op=ALU.add)
                        if lag[0] is not None:
                            lag[0]()
                        lag[0] = do_acc
                        if i == n - 1:
                            lag[0]()
                            lag[0] = None
                        if i == min(1, n - 1):
                            while pending:
                                pending.pop(0)()

                    def tail(j=j, O=O, qbuf=qbuf, q0=q0, cq=cq):
                        k.op("dve", "tensor_tensor", out=accb[j][:, :, :QN], in0=accp[j][0][:, :, :QN], in1=accp[j][1][:, :, :QN], op=ALU.add)
                        for hf_ in range(2):
                            k.op("pe", "matmul", out=psb[6 + hf_][:, :QN], lhsT=self.ones.v, rhs=accb[j][:, hf_, :QN], start=True, stop=True)
                        for hf_ in range(2):
                            ps_ = slice(64 * hf_, 64 * hf_ + 64)
                            k.op("act", "activation", out=den[ps_, :QN], in_=psb[6 + hf_][ps_, :QN], func=AF.Ln, bias=self.esink[ps_, cq:cq + 1])
                        k.op("act", "activation", out=den[:, :QN], in_=den[:, :QN], func=AF.Exp, scale=-1.0)
                        k.op("dve", "tensor_tensor", out=qbuf[:, q0:q0 + QN], in0=O[:, :QN], in1=den[:, :QN], op=ALU.mult)
                    pending.append(tail)
        while pending:
            pending.pop(0)()


def _prep_shared(inp):
    f = lambda a: np.ascontiguousarray(a, dtype=np.float32)
    sh = {}
    wm = inp["w_mod"]
    sh["wmod"] = f(np.stack([wm[l].reshape(8, 128, 36, 256).transpose(2, 1, 0, 3) for l in range(2)]))
    w1, w3, w2 = inp["ffn_w1"], inp["ffn_w3"], inp["ffn_w2"]
    w13 = np.empty((2, 2, NFC, 128, 2, 8, 128), np.float32)
    w2r = np.empty((2, 2, 8, 128, NFC, 128), np.float32)
    for l in range(2):
        for a in range(2):
            w13[l, a, :, :, 0] = w1[l, a].reshape(8, 128, NFC, 128).transpose(2, 1, 0, 3)
            w13[l, a, :, :, 1] = w3[l, a].reshape(8, 128, NFC, 128).transpose(2, 1, 0, 3)
            w2r[l, a] = w2[l, a].reshape(NFC, 128, 8, 128).transpose(2, 1, 0, 3)
    sh["w13"] = w13
    sh["w2"] = w2r
    ew = inp["e_w_in"][0]
    sh["ewin"] = f(ew[:, 0:2048].reshape(8, 128, 16, 128).transpose(2, 1, 0, 3))
    sh["ewv"] = f(ew[:, 2048:2560].reshape(8, 128, 512).transpose(1, 0, 2))
    sh["ewout"] = f(inp["e_w_out"][0].reshape(8, 128, 8, 128).transpose(2, 1, 0, 3))
    lruw = np.zeros((128, 2, 2, 4, 128), np.float32)
    for gi, wsrc in enumerate((inp["e_lru_wa"][0], inp["e_lru_wi"][0])):
        for d in range(2):
            for c in range(4):
                lruw[0:64, gi, d, c, 0:64] = wsrc[d, 2 * c]
                lruw[64:128, gi, d, c, 64:128] = wsrc[d, 2 * c + 1]
    sh["lruw"] = lruw.reshape(128, 16, 128)
    ow = inp["o_w_in"][0]
    qcols = ow[:, 0:1024]
    kcols = ow[:, 1024:1280].reshape(1024, 4, 64)
    kdup = np.concatenate([kcols, kcols], axis=2).reshape(1024, 512)
    fm = np.concatenate([qcols, kdup], axis=1)
    sh["owin"] = f(fm.reshape(8, 128, 12, 128).transpose(2, 1, 0, 3))
    sh["owv"] = f(ow[:, 1280:1536].reshape(8, 128, 256).transpose(1, 0, 2))
    sh["owout"] = f(inp["o_w_out"][0].reshape(8, 128, 8, 128).transpose(2, 1, 0, 3))
    C, S = rope_tables()
    sh["ropec"], sh["ropes"] = C, S
    sh["rmat"] = rot_matrix()
    sh["band"] = band_mask()
    return sh


def _prep_vecs(inp, b):
    v = np.zeros((128, NVEC), np.float32)

    def put(name, arr):
        o, n = VEC[name]
        arr = np.asarray(arr, np.float32).reshape(arr.shape[0], -1)
        assert arr.shape[1] == n, (name, arr.shape, n)
        v[:arr.shape[0], o:o + n] = arr

    put("norm_g", inp["norm_g"].reshape(6, 8, 128).transpose(2, 0, 1))
    put("bmod", inp["b_mod"].reshape(2, 72, 128).transpose(2, 0, 1))
    cond = np.stack([inp["c"][b], inp["c_ctx"]])
    put("cond", cond.reshape(2, 8, 128).transpose(2, 1, 0))
    put("conv_w", inp["e_conv_w"][0].reshape(4, 4, 128).transpose(2, 1, 0))
    put("conv_b", inp["e_conv_b"][0].reshape(4, 128).T)
    put("lru_ba", inp["e_lru_ba"][0].reshape(2, 4, 128).transpose(2, 0, 1))
    put("lru_bi", inp["e_lru_bi"][0].reshape(2, 4, 128).transpose(2, 0, 1))
    put("lru_lam", inp["e_lru_lam"][0].reshape(2, 4, 128).transpose(2, 0, 1))
    put("h0", inp["state_lru"][b, 0].reshape(2, 4, 128).transpose(2, 0, 1))
    put("e_q_g", np.tile(inp["e_q_g"][0], 2)[:, None])
    put("e_k_g", np.tile(inp["e_k_g"][0], 2)[:, None])
    put("subln_g", inp["e_subln_g"][0][:, None])
    put("e_lam", inp["e_lam"][0].T)
    put("o_q_g", np.tile(inp["o_q_g"][0], 2)[:, None])
    put("o_k_g", np.tile(inp["o_k_g"][0], 2)[:, None])
    sk = inp["o_sink"][0].reshape(8, 2)
    put("sink", np.repeat(sk.T, 64, axis=0))
    return v


_PROG = None


def kernel(**inp):
    global _PROG
    inp = {k_: np.asarray(v_) for k_, v_ in inp.items()}
    if _PROG is None:
        p = Prog()
        p.build()
        _PROG = p
    prog = _PROG
    sh = _prep_shared(inp)
    in_maps = []
    f = lambda a: np.ascontiguousarray(a, dtype=np.float32)
    for b in range(8):
        m = dict(sh)
        xs = inp["x_sample"][b].T.reshape(8, 128, TS)
        xp = inp["x_prompt"][4 * b:4 * b + 4].reshape(TP, D).T.reshape(8, 128, TP)
        m["xin"] = f(np.concatenate([xs, xp], axis=2))
        m["vecs"] = _prep_vecs(inp, b)
        m["cdk"] = f(inp["cache_diff_k"][b, 0].transpose(2, 1, 0))
        m["cdv"] = f(inp["cache_diff_v"][b, 0].reshape(4, 128, 512).transpose(1, 0, 2))
        ckw = inp["cache_win_k"][b, 0].transpose(2, 1, 0)
        m["cwk"] = f(np.concatenate([ckw, ckw], axis=0))
        m["cwv"] = f(inp["cache_win_v"][b, 0].reshape(4, 128, 256).transpose(1, 0, 2))
        in_maps.append(m)
    import time as _t, sys as _s
    _t0 = _t.time()
    res = run_bass_kernel_spmd(prog.nc, in_maps, core_ids=list(range(8)))
    print(f"[kernel] launch {_t.time() - _t0:.1f}s", file=_s.stderr)
    R = res.results
    y_prompt = np.empty((32, 256, D), np.float32)
    y_sample = np.empty((8, TS, D), np.float32)
    ndk = np.empty((32, 1, 256, 4, 128), np.float32)
    ndv = np.empty((32, 1, 256, 4, 128), np.float32)
    nst = np.empty((32, 1, 2, 512), np.float32)
    nwk = np.empty((32, 1, 256, 4, 64), np.float32)
    nwv = np.empty((32, 1, 256, 4, 64), np.float32)
    for b in range(8):
        r = R[b]
        yo = r["yout"].reshape(D, TT).T
        y_sample[b] = yo[:TS]
        y_prompt[4 * b:4 * b + 4] = yo[TS:].reshape(4, 256, D)
        ndk[4 * b:4 * b + 4, 0] = r["ndk"].transpose(2, 1, 0).reshape(4, 256, 4, 128)
        ndv[4 * b:4 * b + 4, 0] = r["ndv"].transpose(1, 0, 2).reshape(4, 256, 4, 128)
        nst[4 * b:4 * b + 4, 0] = r["nst"].reshape(128, 4, 2, 4).transpose(1, 2, 3, 0).reshape(4, 2, 512)
        nwk[4 * b:4 * b + 4, 0] = r["nwk"].transpose(2, 1, 0).reshape(4, 256, 4, 64)
        nwv[4 * b:4 * b + 4, 0] = r["nwv"].transpose(1, 0, 2).reshape(4, 256, 4, 64)
    if DEBUG:
        kernel.dbg = [R[b]["dbg"] for b in range(8)]
    return (y_prompt, y_sample, ndk, ndv, nst, nwk, nwv)
```
